# Optimizing a Trainium2 kernel written in Bass

```python
import jax, jax.numpy as jnp
from jax import lax
import numpy as np

D_MODEL = 2048
BATCH = 16
SEQ = 2048
DEPTH = 4
DEC_BATCH = 16
DEC_SEQ = 16
PAST_LEN = 2048

CHUNK = 64
LEFT_CHUNKS = 8
ATT_WINDOW = LEFT_CHUNKS * CHUNK
D_MIX = D_MODEL
D_CONV = D_MIX // 2
N_HEADS = 8
HEAD_DIM = 128
D_ATTN = N_HEADS * HEAD_DIM
D_IN_PROJ = 3 * D_CONV + 3 * D_ATTN
CONV_WIDTH = 3
REL_CLIP = 2 * CHUNK
N_REL = 2 * REL_CLIP + 1
D_FF = ((4 * D_MODEL * 2 // 3) + 127) // 128 * 128
EPS = 1e-6
NEG_INF = -1e30

kernel_name = 'hymba_conformer_streaming_step'


def rmsnorm(x, g):
    xf = x.astype(jnp.float32)
    y = xf * lax.rsqrt(jnp.mean(xf * xf, axis=-1, keepdims=True) + EPS)
    return (y * g.astype(jnp.float32)).astype(x.dtype)


def swiglu(x, w_gate, w_up, w_down):
    return (jax.nn.silu(x @ w_gate) * (x @ w_up)) @ w_down


def rel_bias_lookup(table, dist):
    idx = jnp.clip(dist, -REL_CLIP, REL_CLIP) + REL_CLIP
    return jnp.transpose(table[idx], (2, 0, 1)).astype(jnp.float32)


def attend(q, k, v, bias, valid=None):
    s = jnp.einsum('bqhd,bkhd->bhqk', q, k).astype(jnp.float32) * (HEAD_DIM ** -0.5) + bias
    if valid is not None:
        s = jnp.where(valid, s, NEG_INF)
    p = jax.nn.softmax(s, axis=-1)
    return jnp.einsum('bhqk,bkhd->bqhd', p.astype(v.dtype), v)


def band_attention_prompt(q, k, v, table):
    b, s, h, d = q.shape
    n_chunks = s // CHUNK
    band = ATT_WINDOW + CHUNK
    pad = ((0, 0), (ATT_WINDOW, 0), (0, 0), (0, 0))
    kp = jnp.pad(k, pad)
    vp = jnp.pad(v, pad)
    r = jnp.arange(CHUNK)
    rb = jnp.arange(band)
    bias = rel_bias_lookup(table, r[:, None] + ATT_WINDOW - rb[None, :])

    def one_chunk(c):
        start = c * CHUNK
        qs = lax.dynamic_slice_in_dim(q, start, CHUNK, axis=1)
        ks = lax.dynamic_slice_in_dim(kp, start, band, axis=1)
        vs = lax.dynamic_slice_in_dim(vp, start, band, axis=1)
        valid = (start - ATT_WINDOW + rb) >= 0
        return attend(qs, ks, vs, bias, valid)

    out = lax.map(one_chunk, jnp.arange(n_chunks))
    return jnp.transpose(out, (1, 0, 2, 3, 4)).reshape(b, s, h, d)


def band_attention_sample(q, k_new, v_new, k_cache, v_cache, table):
    t = q.shape[1]
    n_cache = k_cache.shape[1]
    k_all = jnp.concatenate([k_cache, k_new], axis=1)
    v_all = jnp.concatenate([v_cache, v_new], axis=1)
    dist = jnp.arange(t)[:, None] + n_cache - jnp.arange(n_cache + t)[None, :]
    bias = rel_bias_lookup(table, dist)
    return attend(q, k_all, v_all, bias)


def layer(x, conv_state, k_cache, v_cache, p):
    b, s, _ = x.shape
    h = x + 0.5 * rmsnorm(swiglu(rmsnorm(x, p['ln_ffn1_pre']), p['ffn1_w_gate'], p['ffn1_w_up'], p['ffn1_w_down']), p['ln_ffn1_post'])
    u = rmsnorm(h, p['ln_mix_pre'])
    z = u @ p['w_in']
    bg, cg, xc, q, k, v = jnp.split(z, [D_CONV, 2 * D_CONV, 3 * D_CONV, 3 * D_CONV + D_ATTN, 3 * D_CONV + 2 * D_ATTN], axis=-1)
    gx = cg * xc
    if conv_state is None:
        gp = jnp.pad(gx, ((0, 0), (CONV_WIDTH - 1, 0), (0, 0)))
    else:
        gp = jnp.concatenate([conv_state.astype(gx.dtype), gx], axis=1)
    w = p['conv_w']
    conv = gp[:, 0:s] * w[0]
    for j in range(1, CONV_WIDTH):
        conv = conv + gp[:, j:j + s] * w[j]
    y_conv = bg * conv
    new_conv = gp[:, gp.shape[1] - (CONV_WIDTH - 1):]
    q = q.reshape(b, s, N_HEADS, HEAD_DIM)
    k = k.reshape(b, s, N_HEADS, HEAD_DIM)
    v = v.reshape(b, s, N_HEADS, HEAD_DIM)
    if k_cache is None:
        y_attn = band_attention_prompt(q, k, v, p['rel_bias'])
        keep = min(ATT_WINDOW, s)
        new_k = k[:, s - keep:]
        new_v = v[:, s - keep:]
    else:
        y_attn = band_attention_sample(q, k, v, k_cache.astype(k.dtype), v_cache.astype(v.dtype), p['rel_bias'])
        new_k = k
        new_v = v
    y_attn = y_attn.reshape(b, s, D_ATTN)
    mix = jnp.concatenate([rmsnorm(y_conv, p['g_conv_out']), rmsnorm(y_attn, p['g_attn_out'])], axis=-1)
    h = h + rmsnorm(mix @ p['w_out'], p['ln_mix_post'])
    y = h + 0.5 * rmsnorm(swiglu(rmsnorm(h, p['ln_ffn2_pre']), p['ffn2_w_gate'], p['ffn2_w_up'], p['ffn2_w_down']), p['ln_ffn2_post'])
    return y, new_conv, new_k, new_v


def setup_inputs(seed: int = 0) -> dict:
    key = jax.random.key(seed)
    ks = jax.random.split(key, 24)
    f32 = jnp.float32
    att_cache = min(ATT_WINDOW, PAST_LEN)

    def nrm(k, shape, scale):
        return jax.random.normal(k, shape, f32) * scale

    def gain(k, n):
        return 1.0 + 0.05 * jax.random.normal(k, (DEPTH, n), f32)

    return {
        'x_prompt': nrm(ks[0], (BATCH, SEQ, D_MODEL), 1.0),
        'x_sample': nrm(ks[1], (DEC_BATCH, DEC_SEQ, D_MODEL), 1.0),
        'cache_k': nrm(ks[2], (DEPTH, DEC_BATCH, att_cache, N_HEADS, HEAD_DIM), 1.0),
        'cache_v': nrm(ks[3], (DEPTH, DEC_BATCH, att_cache, N_HEADS, HEAD_DIM), 1.0),
        'state_conv': nrm(ks[4], (DEPTH, DEC_BATCH, CONV_WIDTH - 1, D_CONV), 1.0),
        'ln_ffn1_pre': gain(ks[5], D_MODEL),
        'ffn1_w_gate': nrm(ks[6], (DEPTH, D_MODEL, D_FF), D_MODEL ** -0.5),
        'ffn1_w_up': nrm(ks[7], (DEPTH, D_MODEL, D_FF), D_MODEL ** -0.5),
        'ffn1_w_down': nrm(ks[8], (DEPTH, D_FF, D_MODEL), D_FF ** -0.5),
        'ln_ffn1_post': gain(ks[9], D_MODEL),
        'ln_mix_pre': gain(ks[10], D_MODEL),
        'w_in': nrm(ks[11], (DEPTH, D_MODEL, D_IN_PROJ), D_MODEL ** -0.5),
        'conv_w': nrm(ks[12], (DEPTH, CONV_WIDTH, D_CONV), CONV_WIDTH ** -0.5),
        'rel_bias': nrm(ks[13], (DEPTH, N_REL, N_HEADS), 0.5),
        'g_conv_out': gain(ks[14], D_CONV),
        'g_attn_out': gain(ks[15], D_ATTN),
        'w_out': nrm(ks[16], (DEPTH, D_MIX, D_MODEL), D_MIX ** -0.5),
        'ln_mix_post': gain(ks[17], D_MODEL),
        'ln_ffn2_pre': gain(ks[18], D_MODEL),
        'ffn2_w_gate': nrm(ks[19], (DEPTH, D_MODEL, D_FF), D_MODEL ** -0.5),
        'ffn2_w_up': nrm(ks[20], (DEPTH, D_MODEL, D_FF), D_MODEL ** -0.5),
        'ffn2_w_down': nrm(ks[21], (DEPTH, D_FF, D_MODEL), D_FF ** -0.5),
        'ln_ffn2_post': gain(ks[22], D_MODEL),
    }


def reference(x_prompt, x_sample, cache_k, cache_v, state_conv,
              ln_ffn1_pre, ffn1_w_gate, ffn1_w_up, ffn1_w_down, ln_ffn1_post,
              ln_mix_pre, w_in, conv_w, rel_bias, g_conv_out, g_attn_out, w_out, ln_mix_post,
              ln_ffn2_pre, ffn2_w_gate, ffn2_w_up, ffn2_w_down, ln_ffn2_post):
    hp = x_prompt
    hs = x_sample
    kp_l, vp_l, cp_l, ks_l, vs_l, cs_l = [], [], [], [], [], []
    for l in range(DEPTH):
        p = dict(
            ln_ffn1_pre=ln_ffn1_pre[l], ffn1_w_gate=ffn1_w_gate[l], ffn1_w_up=ffn1_w_up[l],
            ffn1_w_down=ffn1_w_down[l], ln_ffn1_post=ln_ffn1_post[l],
            ln_mix_pre=ln_mix_pre[l], w_in=w_in[l], conv_w=conv_w[l], rel_bias=rel_bias[l],
            g_conv_out=g_conv_out[l], g_attn_out=g_attn_out[l], w_out=w_out[l], ln_mix_post=ln_mix_post[l],
            ln_ffn2_pre=ln_ffn2_pre[l], ffn2_w_gate=ffn2_w_gate[l], ffn2_w_up=ffn2_w_up[l],
            ffn2_w_down=ffn2_w_down[l], ln_ffn2_post=ln_ffn2_post[l])
        hp, c_p, k_p, v_p = layer(hp, None, None, None, p)
        hs, c_s, k_s, v_s = layer(hs, state_conv[l], cache_k[l], cache_v[l], p)
        kp_l.append(k_p); vp_l.append(v_p); cp_l.append(c_p)
        ks_l.append(k_s); vs_l.append(v_s); cs_l.append(c_s)
    k_prompt = jnp.stack(kp_l)
    v_prompt = jnp.stack(vp_l)
    conv_prompt = jnp.stack(cp_l)
    k_sample = jnp.stack(ks_l)
    v_sample = jnp.stack(vs_l)
    conv_sample = jnp.stack(cs_l)
    return (hp, hs, k_prompt, v_prompt, conv_prompt, k_sample, v_sample, conv_sample)
```

```python
import contextlib
import numpy as np
import concourse.bass as bass
import concourse.mybir as mybir
from concourse.bass_utils import run_bass_kernel_spmd

F32 = mybir.dt.float32
BF16 = mybir.dt.bfloat16
AF = mybir.ActivationFunctionType
ALU = mybir.AluOpType
AX = mybir.AxisListType

D = 2048
KD = 16
DCONV = 1024
NREL = 257
EPS = 1e-6
NEG = -1e30
SCALE = 128 ** -0.5
NSLOT = 3


class Trk:
    def __init__(self):
        self.ops = {e: [] for e in ("pe", "act", "dve", "pool", "sp")}
        self.cnt = {e: 0 for e in ("pe", "act", "dve", "pool")}
        self.tot = {}
        self.res = {}
        self.seen = {e: {} for e in self.ops}
        self.overlaps = {}
        self.pe_pending = False

    def _r(self, k):
        if k not in self.res:
            self.res[k] = {"w": None, "r": {}}
        return self.res[k]

    def op(self, eng, meth, reads=(), writes=(), inc=True, dma=None, **kw):
        fn = (meth, kw)
        own = eng if eng in self.cnt else None
        waits = {}

        def need(ev, skip_same):
            if ev is None:
                return
            s, v = ev
            if skip_same and s == own and dma is None:
                return
            if self.seen[eng].get(s, 0) >= v:
                return
            if waits.get(s, 0) < v:
                waits[s] = v

        for k in reads:
            need(self._r(k)["w"], False)
        for k in writes:
            for kk in [k] + self.overlaps.get(k, []):
                r = self._r(kk)
                need(r["w"], True)
                for s, v in r["r"].items():
                    need((s, v), True)
        for s, v in waits.items():
            self.seen[eng][s] = v
        rec = {"fn": fn, "waits": list(waits.items()), "inc": None}
        self.ops[eng].append(rec)
        if dma is not None:
            self.tot[dma] = self.tot.get(dma, 0) + 16
            ev = (dma, self.tot[dma])
            rec["inc"] = (dma, 16)
        elif eng == "pe":
            ev = ("pe", self.cnt["pe"] + 1)
            self.pe_pending = True
            if inc:
                self.flush_pe()
        else:
            self.cnt[eng] += 1
            ev = (eng, self.cnt[eng])
            rec["inc"] = (eng, 1)
        for k in writes:
            r = self._r(k)
            r["w"] = ev
            r["r"] = {}
        for k in reads:
            r = self._r(k)
            if r["r"].get(ev[0], 0) < ev[1]:
                r["r"][ev[0]] = ev[1]
        return ev

    def flush_pe(self):
        if self.pe_pending:
            self.ops["pe"][-1]["inc"] = ("pe", 1)
            self.cnt["pe"] += 1
            self.pe_pending = False


def build_program(cfg):
    L = cfg["L"]; DFF = cfg["DFF"]; NF = DFF // 128
    NPS = cfg["NPS"]; SEQ = cfg["SEQ"]; NSS = cfg["NSS"]; TS = cfg["TS"]; CACHE = cfg["CACHE"]
    NT = SEQ // 512
    DIN = 6144
    assert NSS == 2 and TS == 16 and CACHE == 512

    nc = bass.Bass("TRN2", target_bir_lowering=False)

    def din(name, shape, dt=F32):
        return nc.dram_tensor(name, list(shape), dt, kind="ExternalInput")

    def dout(name, shape):
        return nc.dram_tensor(name, list(shape), F32, kind="ExternalOutput")

    xp = din("xp", [NPS, SEQ, D]); xs_in = din("xs", [NSS, TS, D])
    ck = din("ck", [L, NSS, CACHE, 1024]); cv = din("cv", [L, NSS, CACHE, 1024])
    sconv = din("sconv", [L, NSS, 128, 8, 2])
    wg = [din("wg1", [L, D, DFF]), din("wg2", [L, D, DFF])]
    wu = [din("wu1", [L, D, DFF]), din("wu2", [L, D, DFF])]
    wd = [din("wd1", [L, DFF, D]), din("wd2", [L, DFF, D])]
    w_in = din("w_in", [L, D, DIN]); w_out = din("w_out", [L, D, D])
    gpre_d = din("gpre", [128, L * 3 * 16]); gmix_d = din("gmix", [128, L * 16])
    convw_d = din("convw", [128, L * 24]); gpost_d = din("gpost", [L * 3, D])
    bias_d = din("biasx", [L, 128, 8, 640]); ident_d = din("ident", [128, 128])

    yp = dout("yp", [NPS, SEQ, D]); ys = dout("ys", [NSS, TS, D])
    kp = dout("kp", [L, NPS, 512, 1024]); vp = dout("vp", [L, NPS, 512, 1024])
    cp = dout("cp", [L, NPS, 128, 8, 2])
    ksm = dout("ksm", [L, NSS, TS, 1024]); vsm = dout("vsm", [L, NSS, TS, 1024])
    csm = dout("csm", [L, NSS, 128, 8, 2])
    kvs_k = nc.dram_tensor("kvs_k", [L, 128, 8, 512], BF16)
    kvs_v = nc.dram_tensor("kvs_v", [L, 128, 4, 1024], BF16)

    T = Trk()
    es = contextlib.ExitStack()

    def sb(name, shape, dt):
        return es.enter_context(nc.sbuf_tensor(name, list(shape), dt))

    xres = sb("xres", [128, 4, D], F32)
    xnT = sb("xnT", [128, KD, 512], BF16)
    NB = max(NF * 512, 4608 + 8704 + 10240 + 10240)
    regB = sb("regB", [128, NB], BF16)
    regA = sb("regA", [128, 8224], F32)
    wring = sb("wring", [128, NSLOT, 4096], BF16)
    gpost = sb("gpost_s", [128, D], F32)
    xsb = sb("xsb_s", [128, D], BF16)
    tmp = sb("tmp_s", [128, 4, 512], F32)
    Sb = sb("Sb", [128, 2, 640], F32)
    Pb = sb("Pb", [128, 2, 640], BF16)
    PT = sb("PT", [128, 2, 640], BF16)
    sqt = sb("sqt", [128, 2, 512], BF16)
    stat = sb("stat_s", [128, 64], F32)
    gpre = sb("gpre_s", [128, L * 48], F32)
    gmix = sb("gmix_s", [128, L * 16], F32)
    convw = sb("convw_s", [128, L * 24], F32)
    hist = sb("hist_s", [128, L * 32], F32)
    identf = sb("identf", [128, 128], F32)
    ident = sb("ident_s", [128, 128], BF16)
    ones = sb("ones_s", [128, 128], BF16)
    epsc = sb("epsc", [128, 1], F32)
    psb = [es.enter_context(nc.psum_tensor(f"ps{i}", [128, 512], F32)) for i in range(8)]

    hT = regB[:, 0:NF * 512].rearrange("p (j t) -> p j t", j=NF)
    QTp = regB[:, 0:4096].rearrange("p (h t) -> p h t", h=8)
    QTs = regB[:, 0:512].rearrange("p (h t) -> p h t", h=8)
    kstage = regB[:, 512:4608].rearrange("p (s e) -> p s e", s=4)
    KTcat = regB[:, 4608:4608 + 8704].rearrange("p (h k) -> p h k", h=8)
    Vcat = regB[:, 13312:13312 + 10240].rearrange("p (c e) -> p c e", c=10)
    biasv = regB[:, 23552:23552 + 10240].bitcast(F32).rearrange("p (h j) -> p h j", h=8)
    fbuf = regA[:, 0:8192].rearrange("p (s d) -> p s d", s=4)
    yc = regA[:, 4112:8208].rearrange("p (c t) -> p c t", c=8)
    ya = regA[:, 0:4096].rearrange("p (g e) -> p g e", g=4)

    mixkeys = ["QT", "KTcat", "Vcat", "bias", "kstage"]
    T.overlaps["hT"] = list(mixkeys)
    for k in mixkeys:
        T.overlaps[k] = ["hT"]
    T.overlaps["QT"].append("kstage"); T.overlaps["kstage"].append("QT")
    akeys = [f"gx{c}" for c in range(8)] + [f"yc{c}" for c in range(8)] + [f"ya{g}" for g in range(4)]
    for s in range(4):
        T.overlaps[f"fbuf{s}"] = list(akeys)
    for k in akeys:
        T.overlaps[k] = [f"fbuf{s}" for s in range(4)]
    for g in range(4):
        T.overlaps[f"ya{g}"] += [f"gx{c}" for c in range(8)]
    for c in range(8):
        T.overlaps[f"gx{c}"] += [f"ya{g}" for g in range(4)]

    st_i = [0]

    def statcol():
        st_i[0] = (st_i[0] + 1) % 60
        return st_i[0], f"st{st_i[0]}"

    bank_i = [0]

    def bank():
        bank_i[0] = (bank_i[0] + 1) % 8
        return psb[bank_i[0]], f"ps{bank_i[0]}"

    tmp_i = [0]

    def gettmp():
        tmp_i[0] = (tmp_i[0] + 1) % 4
        return tmp[:, tmp_i[0], :], f"tmp{tmp_i[0]}"

    rot = {"S": 0, "P": 0, "PT": 0, "sq": 0}

    def rot2(k):
        rot[k] ^= 1
        return rot[k]

    def dma_sp(out, in_, sem, reads, writes):
        T.op("sp", "dma_start", out=out, in_=in_, reads=reads, writes=writes, dma=sem)

    def dma_pool(out, in_, sem, reads, writes):
        T.op("pool", "dma_start", out=out, in_=in_, reads=reads, writes=writes, dma=sem)

    wq = []
    wstate = {"issued": 0, "used": 0}

    def w_issue_upto(n):
        while wstate["issued"] < min(n, len(wq)):
            u = wstate["issued"]
            slot = u % NSLOT
            for (dst, src) in wq[u]:
                dma_pool(dst(slot), src, f"w{slot}", reads=(), writes=(f"ws{slot}",))
            wstate["issued"] += 1

    def w_next():
        u = wstate["used"]
        w_issue_upto(u + NSLOT)
        wstate["used"] += 1
        slot = u % NSLOT
        return slot, f"ws{slot}"

    def plan_chunks2(srcs):
        def mk(i):
            return lambda slot: wring[:, slot, i * 2048:(i + 1) * 2048].rearrange("p (k c) -> p k c", k=16)
        wq.append([(mk(i), s) for i, s in enumerate(srcs)])

    def plan_slab(src, nk):
        wq.append([(lambda slot: wring[:, slot, 0:nk * 512].rearrange("p (k c) -> p k c", k=nk), src)])

    def colchunk(w, l, col0):
        return w[l].rearrange("(k p) f -> p k f", p=128)[:, :, col0:col0 + 128]

    def slabsrc(w, l, k0, nk, col0):
        return w[l].rearrange("(k p) f -> p k f", p=128)[:, k0:k0 + nk, col0:col0 + 512]

    INPROJ = [("C", c) for c in range(8)]
    INPROJ = [x for c in range(8) for x in (("C", c), ("xc", c))] + [("B", c) for c in range(8)] + \
             [("q", c) for c in range(8)] + [("k", c) for c in range(8)]
    COL0 = {"B": 0, "C": 1024, "xc": 2048, "q": 3072, "k": 4096}

    STOP = cfg.get("stop")

    def plan_tile_layer(l, need_k):
        for f in range(2):
            if f == 1 and STOP == "ffn1":
                return
            if f == 1:
                for i in range(0, len(INPROJ), 2):
                    plan_chunks2([colchunk(w_in, l, COL0[kind] + c * 128) for kind, c in INPROJ[i:i + 2]])
                if STOP == "mixA":
                    return
                for sec in ([5120] + ([4096] if need_k else [])):
                    for c2 in range(2):
                        for u in range(2):
                            plan_slab(slabsrc(w_in, l, 8 * u, 8, sec + c2 * 512), 8)
                if STOP in ("mixB", "mixBn", "mixC", "mixD"):
                    return
                for c in range(4):
                    for u in range(2):
                        plan_slab(slabsrc(w_out, l, 8 * u, 8, c * 512), 8)
                if STOP == "mix":
                    return
            for j in range(NF):
                plan_chunks2([colchunk(wg[f], l, j * 128), colchunk(wu[f], l, j * 128)])
            for c in range(4):
                for u in range((NF + 7) // 8):
                    nk = min(8, NF - 8 * u)
                    plan_slab(slabsrc(wd[f], l, 8 * u, nk, c * 512), nk)

    def rstd_from(ss_c, ss_k, n, rows):
        c1, k1 = statcol()
        T.op("act", "activation", out=stat[:rows, c1:c1 + 1], in_=stat[:rows, ss_c:ss_c + 1], func=AF.Sqrt,
             bias=epsc[:rows, 0:1], scale=1.0 / n, reads=(ss_k, "epsc"), writes=(k1,))
        c2, k2 = statcol()
        T.op("dve", "reciprocal", out=stat[:rows, c2:c2 + 1], in_=stat[:rows, c1:c1 + 1],
             reads=(k1,), writes=(k2,))
        return c2, k2

    def sumsq(src_ap, src_keys, rows, width):
        c0, k0 = statcol()
        T.op("dve", "memset", ap=stat[:rows, c0:c0 + 1], constant=0.0, writes=(k0,))
        T.op("act", "activation", out=xsb[:rows, 0:width], in_=src_ap, func=AF.Square,
                                           accum_out=stat[:rows, c0:c0 + 1],
             reads=tuple(src_keys) + (k0,), writes=("xsb", k0))
        return c0, k0

    def transpose_to_xnT(rows, nchunk, kbase, tok0, gain_ap):
        pb_, pk = bank()
        pv = pb_[:, :].bitcast(BF16).rearrange("p (k t) -> p k t", k=8)
        for kk in range(nchunk):
            T.op("pe", "transpose", out=pv[:, kk, :rows], in_=xsb[:rows, kk * 128:(kk + 1) * 128],
                                                   identity=ident[:rows, :rows],
                 reads=("xsb", "ident"), writes=(pk,), inc=(kk == nchunk - 1))
        g_b = gain_ap.unsqueeze(2).to_broadcast([128, nchunk, rows])
        T.op("dve", "tensor_tensor", out=xnT[:, kbase:kbase + nchunk, tok0:tok0 + rows],
                                              in0=pv[:, 0:nchunk, :rows], in1=g_b, op=ALU.mult,
             reads=(pk, "consts"), writes=("xnT",))

    def prenorm(tl, l, gi):
        for s in range(tl["NS"]):
            rows = tl["PS"]
            c0, k0 = sumsq(xres[:rows, s, :], (f"xres{s}",), rows, D)
            c2, k2 = rstd_from(c0, k0, D, rows)
            for half in range(2):
                T.op("dve", "tensor_scalar", out=xsb[:rows, 0:1024], in0=xres[:rows, s, half * 1024:(half + 1) * 1024],
                    scalar1=stat[:rows, c2:c2 + 1], scalar2=None, op0=ALU.mult,
                     reads=(f"xres{s}", k2), writes=("xsb",))
                gb = (l * 3 + gi) * 16 + half * 8
                transpose_to_xnT(rows, 8, half * 8, s * 128, gpre[:, gb:gb + 8])

    def load_gpost(l, gi):
        dma_sp(gpost[:, :], gpost_d[l * 3 + gi:l * 3 + gi + 1, :].partition_broadcast(128), "gp", (), ("gpost",))

    def postnorm(tl, coef=0.5):
        rows = tl["PS"]
        for s in range(tl["NS"]):
            c0, k0 = sumsq(fbuf[:rows, s, :], (f"fbuf{s}",), rows, D)
            c2, k2 = rstd_from(c0, k0, D, rows)
            T.op("dve", "scalar_tensor_tensor", out=fbuf[:rows, s, :], in0=fbuf[:rows, s, :],
                                                         scalar=stat[:rows, c2:c2 + 1], in1=gpost[:rows, :],
                                                         op0=ALU.mult, op1=ALU.mult,
                 reads=(f"fbuf{s}", k2, "gpost"), writes=(f"fbuf{s}",))
            T.op("dve", "scalar_tensor_tensor", out=xres[:rows, s, :], in0=fbuf[:rows, s, :],
                                                         scalar=coef, in1=xres[:rows, s, :],
                                                         op0=ALU.mult, op1=ALU.add,
                 reads=(f"fbuf{s}", f"xres{s}"), writes=(f"xres{s}",))

    evac_i = [0]

    def evac_copy(out, in_, reads, writes):
        evac_i[0] ^= 1
        if evac_i[0]:
            T.op("act", "copy", out=out, in_=in_, reads=reads, writes=writes)
        else:
            T.op("dve", "tensor_copy", out=out, in_=in_, reads=reads, writes=writes)

    def slab_matmul(tl, lhs_fn, lhs_key, nktot, col_slabs):
        rows = tl["PS"]
        for c in range(col_slabs):
            accs = [bank() for _ in range(tl["NS"])]
            k = 0
            while k < nktot:
                nk = min(8, nktot - k)
                slot, wk = w_next()
                wv = wring[:, slot, 0:nk * 512].rearrange("p (k c) -> p k c", k=nk)
                for kk in range(nk):
                    for s in range(tl["NS"]):
                        kq = k + kk
                        last = (kq == nktot - 1)
                        T.op("pe", "matmul", out=accs[s][0][:rows, :], lhsT=lhs_fn(kq, s), rhs=wv[:, kk, :],
                            start=(kq == 0), stop=last,
                             reads=(lhs_key, wk), writes=(accs[s][1],), inc=last)
                T.flush_pe()
                k += nk
            for s in range(tl["NS"]):
                evac_copy(fbuf[:rows, s, c * 512:(c + 1) * 512], accs[s][0][:rows, :],
                          (accs[s][1],), (f"fbuf{s}",))

    def ffn(tl, l, f):
        Tn = tl["T"]
        load_gpost(l, 0 if f == 0 else 2)
        prenorm(tl, l, 0 if f == 0 else 2)
        for j in range(NF):
            slot, wk = w_next()
            wv = wring[:, slot, :].rearrange("p (g k c) -> p g k c", g=2, k=16)
            pg, pgk = bank(); pu, puk = bank()
            for g, (pp, ppk) in enumerate(((pg, pgk), (pu, puk))):
                for k in range(KD):
                    T.op("pe", "matmul", out=pp[:, :Tn], lhsT=wv[:, g, k, :], rhs=xnT[:, k, :Tn],
                                                                 start=(k == 0), stop=(k == KD - 1),
                         reads=("xnT", wk), writes=(ppk,), inc=(k == KD - 1))
            tv, tk = gettmp()
            T.op("act", "activation", out=tv[:, :Tn], in_=pg[:, :Tn], func=AF.Silu,
                 reads=(pgk,), writes=(tk,))
            T.op("dve", "tensor_tensor", out=hT[:, j, :Tn], in0=pu[:, :Tn], in1=tv[:, :Tn], op=ALU.mult,
                 reads=(puk, tk), writes=("hT",))
        slab_matmul(tl, lambda kq, s: hT[:, kq, s * 128:s * 128 + tl["PS"]], "hT", NF, 4)
        postnorm(tl)

    def mix(tl, l):
        Tn = tl["T"]; nseq = tl["nseq"]; Ls = tl["Lseg"]; samp = tl["samp"]
        QT = QTs if samp else QTp
        Wg = 2 + Ls
        gx = regA[:, 0:8 * nseq * Wg].rearrange("p (c a t) -> p c a t", c=8, a=nseq)
        load_gpost(l, 1)
        prenorm(tl, l, 1)
        dma_sp(biasv, bias_d[l], "bias", (), ("bias",))
        T.op("dve", "memset", ap=biasv[0:64, :, 576:640], constant=NEG, writes=("bias",))
        T.op("dve", "memset", ap=biasv[64:128, :, 0:64], constant=NEG, writes=("bias",))
        if samp:
            T.op("dve", "memset", ap=biasv[0:32, :, 528:544], constant=NEG, writes=("bias",))
        hb = l * 32
        hv = hist[:, hb:hb + 32].rearrange("p (a c r) -> p a c r", a=2, c=8)
        if samp:
            for a in range(nseq):
                dma_pool(Vcat[:, 5 * a:5 * a + 4, :], cv[l, a].rearrange("(s p) e -> p s e", p=128), "cv", (), ("Vcat",))
                dma_pool(kstage, ck[l, a].rearrange("(s p) e -> p s e", p=128), "ck", (), ("kstage",))
                for h in range(8):
                    if h % 2 == 0:
                        pb_, pk = bank()
                        pv = pb_[:, :].bitcast(BF16).rearrange("p (k t) -> p k t", k=8)
                    for s4 in range(4):
                        T.op("pe", "transpose", out=pv[:, (h % 2) * 4 + s4, :], in_=kstage[:, s4, h * 128:(h + 1) * 128], identity=ident[:, :],
                             reads=("kstage", "ident"), writes=(pk,), inc=(h % 2 == 1 and s4 == 3))
                    if h % 2 == 1:
                        evac_copy(KTcat[:, h - 1:h + 1, a * 544:a * 544 + 512],
                                  pv[:, :, :].rearrange("p (h s) t -> p h (s t)", h=2), (pk,), ("KTcat",))
                dma_sp(hv[:, a, :, :], sconv[l, a], "hist", (), ("hist",))
        elif tl["ti"] > 0:
            dma_sp(KTcat[:, :, 0:512], kvs_k[l], "kvlk", (f"kvsk{l}",), ("KTcat",))
            dma_sp(Vcat[:, 0:4, :], kvs_v[l], "kvlv", (f"kvsv{l}",), ("Vcat",))
        else:
            T.op("dve", "memset", ap=hist[:, hb:hb + 32], constant=0.0, writes=("hist",))
        cwv = convw[:, l * 24:(l + 1) * 24].rearrange("p (c j) -> p c j", c=8)
        ctmp = {}
        for i, (kind, c) in enumerate(INPROJ):
            if i % 2 == 0:
                slot, wk = w_next()
                wv = wring[:, slot, :].rearrange("p (g k c) -> p g k c", g=2, k=16)
            g = i % 2
            pp, ppk = bank()
            for k in range(KD):
                T.op("pe", "matmul", out=pp[:, :Tn], lhsT=wv[:, g, k, :], rhs=xnT[:, k, :Tn],
                                                                    start=(k == 0), stop=(k == KD - 1),
                     reads=("xnT", wk), writes=(ppk,), inc=(k == KD - 1))
            if kind == "C":
                tv, tk = gettmp()
                ctmp[c] = (tv, tk)
                T.op("act", "copy", out=tv[:, :Tn], in_=pp[:, :Tn], reads=(ppk,), writes=(tk,))
            elif kind == "xc":
                tv, tk = ctmp[c]
                for a in range(nseq):
                    T.op("act", "copy", out=gx[:, c, a, 0:2], in_=hv[:, a, c, :],
                         reads=("hist",), writes=(f"gx{c}",))
                    T.op("dve", "tensor_tensor", out=gx[:, c, a, 2:2 + Ls], in0=pp[:, a * Ls:(a + 1) * Ls], in1=tv[:, a * Ls:(a + 1) * Ls], op=ALU.mult,
                         reads=(ppk, tk), writes=(f"gx{c}",))
                    ycv = yc[:, c, a * Ls:(a + 1) * Ls]
                    T.op("dve", "tensor_scalar", out=ycv, in0=gx[:, c, a, 0:Ls], scalar1=cwv[:, c, 0:1], scalar2=None, op0=ALU.mult,
                         reads=(f"gx{c}", "consts"), writes=(f"yc{c}",))
                    for jj in (1, 2):
                        T.op("dve", "scalar_tensor_tensor", out=ycv, in0=gx[:, c, a, jj:jj + Ls], scalar=cwv[:, c, jj:jj + 1], in1=ycv,
                            op0=ALU.mult, op1=ALU.add,
                             reads=(f"gx{c}", f"yc{c}", "consts"), writes=(f"yc{c}",))
                if c == 7:
                    for a in range(nseq):
                        for cc in range(8):
                            T.op("act", "copy", out=hv[:, a, cc, :], in_=gx[:, cc, a, tl["Lreal"]:tl["Lreal"] + 2],
                                 reads=(f"gx{cc}",), writes=("hist",))
                    if samp:
                        for a in range(nseq):
                            dma_sp(csm[l, a], hv[:, a, :, :], "cst", ("hist",), ())
                    elif tl["last"]:
                        dma_sp(cp[l, tl["b"]], hv[:, 0, :, :], "cst", ("hist",), ())
            elif kind == "B":
                T.op("dve", "tensor_tensor", out=yc[:, c, :Tn], in0=pp[:, :Tn], in1=yc[:, c, :Tn], op=ALU.mult,
                     reads=(ppk, f"yc{c}"), writes=(f"yc{c}",))
            elif kind == "q":
                T.op("act", "copy", out=QT[:, c, :Tn], in_=pp[:, :Tn], reads=(ppk,), writes=("QT",))
            else:
                if samp:
                    for a in range(nseq):
                        T.op("act", "copy", out=KTcat[:, c, a * 544 + 512:a * 544 + 544],
                                                                      in_=pp[:, a * Ls:(a + 1) * Ls],
                             reads=(ppk,), writes=("KTcat",))
                else:
                    T.op("act", "copy", out=KTcat[:, c, 512:1024], in_=pp[:, :Tn],
                         reads=(ppk,), writes=("KTcat",))
        if STOP == "mixA":
            return
        groups = tl["groups"]
        need_out = samp or tl["last"]
        for sec in (["v"] + (["k"] if need_out else [])):
            for c2 in range(2):
                accs = [bank() for _ in groups]
                for u in range(2):
                    slot, wk = w_next()
                    wv = wring[:, slot, :].rearrange("p (k c) -> p k c", k=8)
                    for kk in range(8):
                        kq = 8 * u + kk
                        for gi, (t0, nr) in enumerate(groups):
                            T.op("pe", "matmul", out=accs[gi][0][:nr, :], lhsT=xnT[:, kq, t0:t0 + nr], rhs=wv[:, kk, :],
                                start=(kq == 0), stop=(kq == KD - 1),
                                 reads=("xnT", wk), writes=(accs[gi][1],), inc=(kq == KD - 1))
                    T.flush_pe()
                for gi, (t0, nr) in enumerate(groups):
                    pa, pak = accs[gi]
                    vch = (5 * gi + 4) if samp else (4 + gi)
                    outp = need_out and STOP != "mixBn"
                    if outp:
                        tv, tk = gettmp()
                        T.op("act", "copy", out=tv[:nr, :], in_=pa[:nr, :], reads=(pak,), writes=(tk,))
                        if sec == "v":
                            T.op("dve", "tensor_copy", out=Vcat[:nr, vch, c2 * 512:(c2 + 1) * 512], in_=tv[:nr, :],
                                 reads=(tk,), writes=("Vcat",))
                    elif sec == "v":
                        T.op("act", "copy", out=Vcat[:nr, vch, c2 * 512:(c2 + 1) * 512], in_=pa[:nr, :],
                             reads=(pak,), writes=("Vcat",))
                    if outp:
                        if samp:
                            nr = TS
                            dst = (vsm if sec == "v" else ksm)[l, gi, :, c2 * 512:(c2 + 1) * 512]
                        else:
                            dst = (vp if sec == "v" else kp)[l, tl["b"], t0:t0 + nr, c2 * 512:(c2 + 1) * 512]
                        dma_sp(dst, tv[:nr, :], "ost", (tk,), ("ostq",))
        if (not samp) and (not tl["last"]):
            dma_sp(kvs_k[l], KTcat[:, :, 512:1024], "kvstk", ("KTcat",), (f"kvsk{l}",))
            dma_sp(kvs_v[l], Vcat[:, 4:8, :], "kvstv", ("Vcat",), (f"kvsv{l}",))
        if STOP in ("mixB", "mixBn"):
            return
        pn, pnk = bank()
        for c in range(8):
            r = rot2("sq")
            T.op("act", "activation", out=sqt[:, r, :Tn], in_=yc[:, c, :Tn], func=AF.Square,
                 reads=(f"yc{c}",), writes=(f"sq{r}",))
            T.op("pe", "matmul", out=pn[:, :Tn], lhsT=ones[:, :], rhs=sqt[:, r, :Tn], start=(c == 0), stop=(c == 7),
                 reads=(f"sq{r}", "ones"), writes=(pnk,), inc=True)
        rv, rk = gettmp()
        T.op("act", "activation", out=rv[:, :Tn], in_=pn[:, :Tn], func=AF.Sqrt, bias=epsc[:, 0:1], scale=1.0 / DCONV,
             reads=(pnk, "epsc"), writes=(rk,))
        T.op("dve", "reciprocal", out=rv[:, :Tn], in_=rv[:, :Tn], reads=(rk,), writes=(rk,))
        for c in range(8):
            T.op("dve", "scalar_tensor_tensor", out=xnT[:, c, :Tn], in0=yc[:, c, :Tn],
                                                            scalar=gmix[:, l * 16 + c:l * 16 + c + 1], in1=rv[:, :Tn],
                                                            op0=ALU.mult, op1=ALU.mult,
                 reads=(f"yc{c}", rk, "consts"), writes=("xnT",))
        if STOP == "mixC":
            return
        for gi, (t0, nq) in enumerate(groups):
            if samp:
                ks, ke, jo = 544 * gi, 544 * gi + 544, 0
            else:
                ks = 512 if tl["ti"] == 0 else 128 * gi
                ke = 128 * gi + 640
                jo = ks - 128 * gi
            nk = ke - ks
            n1 = min(512, nk); n2 = nk - n1
            nch = (nk + 127) // 128
            for h in range(8):
                b1, b1k = bank()
                T.op("pe", "matmul", out=b1[:nq, :n1], lhsT=QT[:, h, t0:t0 + nq], rhs=KTcat[:, h, ks:ks + n1],
                                                        start=True, stop=True,
                     reads=("QT", "KTcat"), writes=(b1k,), inc=True)
                if n2:
                    b2, b2k = bank()
                    T.op("pe", "matmul", out=b2[:nq, :n2], lhsT=QT[:, h, t0:t0 + nq], rhs=KTcat[:, h, ks + n1:ke],
                                                            start=True, stop=True,
                         reads=("QT", "KTcat"), writes=(b2k,), inc=True)
                r = rot2("S")
                T.op("dve", "scalar_tensor_tensor", out=Sb[:nq, r, 0:n1], in0=b1[:nq, :n1], scalar=SCALE, in1=biasv[:nq, h, jo:jo + n1],
                    op0=ALU.mult, op1=ALU.add, reads=(b1k, "bias"), writes=(f"S{r}",))
                if n2:
                    T.op("dve", "scalar_tensor_tensor", out=Sb[:nq, r, n1:nk], in0=b2[:nq, :n2], scalar=SCALE, in1=biasv[:nq, h, jo + n1:jo + nk],
                        op0=ALU.mult, op1=ALU.add, reads=(b2k, "bias"), writes=(f"S{r}",))
                cm, km = statcol()
                T.op("dve", "reduce_max", out=stat[:nq, cm:cm + 1], in_=Sb[:nq, r, 0:nk], axis=AX.X,
                     reads=(f"S{r}",), writes=(km,))
                cn, kn = statcol()
                T.op("dve", "tensor_scalar", out=stat[:nq, cn:cn + 1], in0=stat[:nq, cm:cm + 1],
                                                                  scalar1=-1.0, scalar2=None, op0=ALU.mult,
                     reads=(km,), writes=(kn,))
                csu, ksu = statcol()
                T.op("dve", "memset", ap=stat[:nq, csu:csu + 1], constant=0.0, writes=(ksu,))
                rp = rot2("P")
                T.op("act", "activation", out=Pb[:nq, rp, 0:nk], in_=Sb[:nq, r, 0:nk], func=AF.Exp, bias=stat[:nq, cn:cn + 1], scale=1.0,
                    accum_out=stat[:nq, csu:csu + 1], reads=(f"S{r}", kn, ksu), writes=(f"P{rp}", ksu))
                pb_, pk = bank()
                pv = pb_[:, :].bitcast(BF16).rearrange("p (k t) -> p k t", k=8)
                for ci in range(nch):
                    csz = min(128, nk - ci * 128)
                    T.op("pe", "transpose", out=pv[:csz, ci, :nq], in_=Pb[:nq, rp, ci * 128:ci * 128 + csz], identity=ident[:nq, :nq],
                         reads=(f"P{rp}", "ident"), writes=(pk,), inc=(ci == nch - 1))
                rt = rot2("PT")
                PTv = PT[:, rt, :].rearrange("p (k t) -> p k t", k=5)
                nfull = nk // 128
                T.op("act", "copy", out=PTv[:, 0:nfull, :nq], in_=pv[:, 0:nfull, :nq],
                     reads=(pk,), writes=(f"PT{rt}",))
                if nfull < nch:
                    cs = nk - nfull * 128
                    T.op("act", "copy", out=PTv[:cs, nfull, :nq], in_=pv[:cs, nfull, :nq],
                         reads=(pk,), writes=(f"PT{rt}",))
                po, pok = bank()
                for ci in range(nch):
                    csz = min(128, nk - ci * 128)
                    vch = (5 * gi + ci) if samp else (ks // 128 + ci)
                    T.op("pe", "matmul", out=po[:nq, 0:128], lhsT=PTv[:csz, ci, :nq], rhs=Vcat[:csz, vch, h * 128:(h + 1) * 128],
                        start=(ci == 0), stop=(ci == nch - 1),
                         reads=(f"PT{rt}", "Vcat"), writes=(pok,), inc=(ci == nch - 1))
                cr, kr = statcol()
                T.op("dve", "reciprocal", out=stat[:nq, cr:cr + 1], in_=stat[:nq, csu:csu + 1],
                     reads=(ksu,), writes=(kr,))
                T.op("dve", "tensor_scalar", out=ya[:nq, gi, h * 128:(h + 1) * 128], in0=po[:nq, 0:128], scalar1=stat[:nq, cr:cr + 1],
                    scalar2=None, op0=ALU.mult, reads=(pok, kr), writes=(f"ya{gi}",))
            c0, k0 = sumsq(ya[:nq, gi, :], (f"ya{gi}",), nq, 1024)
            c2, k2 = rstd_from(c0, k0, 1024, nq)
            T.op("dve", "tensor_scalar", out=xsb[:nq, 0:1024], in0=ya[:nq, gi, :],
                                                       scalar1=stat[:nq, c2:c2 + 1], scalar2=None, op0=ALU.mult,
                 reads=(f"ya{gi}", k2), writes=("xsb",))
            transpose_to_xnT(nq, 8, 8, t0, gmix[:, l * 16 + 8:l * 16 + 16])
        if STOP == "mixD":
            return
        slab_matmul(tl, lambda kq, s: xnT[:, kq, s * 128:s * 128 + tl["PS"]], "xnT", KD, 4)
        postnorm(tl, 1.0)

    tiles = []
    for b in range(NPS):
        for ti in range(NT):
            tiles.append(dict(samp=False, b=b, ti=ti, last=(ti == NT - 1), T=512, NS=4, PS=128, nseq=1, Lseg=512, Lreal=512,
                              groups=[(i * 128, 128) for i in range(4)]))
    tiles.append(dict(samp=True, b=0, ti=0, last=True, T=64, NS=1, PS=64, nseq=NSS, Lseg=32, Lreal=TS,
                      groups=[(a * 32, 32) for a in range(NSS)]))
    for tl in tiles:
        for l in range(L):
            plan_tile_layer(l, tl["samp"] or tl["last"])

    dma_sp(identf[:, :], ident_d[:, :], "cst0", (), ("consts",))
    dma_sp(gpre[:, :], gpre_d[:, :], "cst0", (), ("consts",))
    dma_sp(gmix[:, :], gmix_d[:, :], "cst0", (), ("consts",))
    dma_sp(convw[:, :], convw_d[:, :], "cst0", (), ("consts",))
    T.op("dve", "tensor_copy", out=ident[:, :], in_=identf[:, :], reads=("consts",), writes=("ident",))
    T.op("dve", "memset", ap=ones[:, :], constant=1.0, writes=("ones",))
    T.op("dve", "memset", ap=epsc[:, :], constant=EPS, writes=("epsc",))

    for tl in tiles:
        xk = tuple(f"xres{s}" for s in range(tl["NS"]))
        if tl["samp"]:
            T.op("dve", "memset", ap=xres[:64, 0, :], constant=0.0, writes=xk)
            for a in range(NSS):
                dma_sp(xres[32 * a:32 * a + TS, 0, :], xs_in[a], "ldx", (), xk)
        else:
            t0 = tl["ti"] * 512
            dma_sp(xres[:, :, :], xp[tl["b"], t0:t0 + 512, :].rearrange("(s p) d -> p s d", p=128), "ldx", (), xk)
        for l in range(L):
            ffn(tl, l, 0)
            if STOP != "ffn1":
                mix(tl, l)
                if STOP is None:
                    ffn(tl, l, 1)
        if tl["samp"]:
            for a in range(NSS):
                dma_sp(ys[a], xres[32 * a:32 * a + TS, 0, :], "sty", xk, ())
        else:
            t0 = tl["ti"] * 512
            dma_sp(yp[tl["b"], t0:t0 + 512, :].rearrange("(s p) d -> p s d", p=128), xres[:, :, :], "sty", xk, ())
    assert wstate["used"] == len(wq), (wstate, len(wq))
    T.flush_pe()

    semnames = set(T.cnt) | set(T.tot)
    sems = {n: es.enter_context(nc.semaphore("s_" + n)) for n in sorted(semnames)}
    engs = {"pe": "tensor", "act": "scalar", "dve": "vector", "pool": "gpsimd", "sp": "sync"}
    with nc.Block() as block:
        for en, attr in engs.items():
            def body(e, en=en):
                for rec in T.ops[en]:
                    for s, v in rec["waits"]:
                        e.wait_ge(sems[s], v)
                    ins = getattr(e, rec["fn"][0])(**rec["fn"][1])
                    if rec["inc"] is not None:
                        ins.then_inc(sems[rec["inc"][0]], rec["inc"][1])
                if en == "sp":
                    for s, v in T.tot.items():
                        e.wait_ge(sems[s], v)
            getattr(block, attr)(body)
    es.close()
    return nc


def host_inputs(inp, cfg, core):
    L = cfg["L"]; NPS = cfg["NPS"]; NSS = cfg["NSS"]
    f = lambda a: np.ascontiguousarray(np.asarray(a, dtype=np.float32))
    m = {}
    m["xp"] = f(inp["x_prompt"][core * NPS:(core + 1) * NPS])
    m["xs"] = f(inp["x_sample"][core * NSS:(core + 1) * NSS])
    m["ck"] = f(np.asarray(inp["cache_k"])[:, core * NSS:(core + 1) * NSS].reshape(L, NSS, 512, 1024))
    m["cv"] = f(np.asarray(inp["cache_v"])[:, core * NSS:(core + 1) * NSS].reshape(L, NSS, 512, 1024))
    sc = np.asarray(inp["state_conv"])[:, core * NSS:(core + 1) * NSS]
    m["sconv"] = f(sc.reshape(L, NSS, 2, 8, 128).transpose(0, 1, 4, 3, 2))
    return m


def shared_inputs(inp, cfg):
    L = cfg["L"]
    f = lambda a: np.ascontiguousarray(np.asarray(a, dtype=np.float32))
    m = {}
    m["wg1"] = f(inp["ffn1_w_gate"]); m["wu1"] = f(inp["ffn1_w_up"]); m["wd1"] = f(inp["ffn1_w_down"])
    m["wg2"] = f(inp["ffn2_w_gate"]); m["wu2"] = f(inp["ffn2_w_up"]); m["wd2"] = f(inp["ffn2_w_down"])
    m["w_in"] = f(inp["w_in"]); m["w_out"] = f(inp["w_out"])
    pre = np.stack([np.asarray(inp["ln_ffn1_pre"]), np.asarray(inp["ln_mix_pre"]), np.asarray(inp["ln_ffn2_pre"])], 1)
    m["gpre"] = f(pre.reshape(L, 3, 16, 128).transpose(3, 0, 1, 2).reshape(128, L * 48))
    gm = np.concatenate([np.asarray(inp["g_conv_out"]), np.asarray(inp["g_attn_out"])], 1)
    m["gmix"] = f(gm.reshape(L, 16, 128).transpose(2, 0, 1).reshape(128, L * 16))
    cw = np.asarray(inp["conv_w"])
    m["convw"] = f(cw.reshape(L, 3, 8, 128).transpose(3, 0, 2, 1).reshape(128, L * 24))
    post = np.stack([np.asarray(inp["ln_ffn1_post"]), np.asarray(inp["ln_mix_post"]), np.asarray(inp["ln_ffn2_post"])], 1)
    m["gpost"] = f(post.reshape(L * 3, D))
    q = np.arange(128)[:, None]; j = np.arange(640)[None, :]
    idx = np.clip(q + 512 - j, -128, 128) + 128
    rb = np.asarray(inp["rel_bias"])
    m["biasx"] = f(rb[:, idx, :].transpose(0, 1, 3, 2))
    m["ident"] = np.eye(128, dtype=np.float32)
    return m


_CACHE = {}


def run(inp, cfg, n_cores):
    key = tuple(sorted((k, str(v)) for k, v in cfg.items()))
    if key not in _CACHE:
        _CACHE[key] = build_program(cfg)
    nc = _CACHE[key]
    sh = shared_inputs(inp, cfg)
    in_maps = []
    for c in range(n_cores):
        m = dict(sh)
        m.update(host_inputs(inp, cfg, c))
        in_maps.append(m)
    res = run_bass_kernel_spmd(nc, in_maps, core_ids=list(range(n_cores)))
    R = res.results
    L = cfg["L"]
    cat = lambda k, ax: np.concatenate([r[k] for r in R], axis=ax)
    y_p = cat("yp", 0); y_s = cat("ys", 0)
    k_p = cat("kp", 1).reshape(L, -1, 512, 8, 128); v_p = cat("vp", 1).reshape(L, -1, 512, 8, 128)
    k_s = cat("ksm", 1).reshape(L, -1, cfg["TS"], 8, 128); v_s = cat("vsm", 1).reshape(L, -1, cfg["TS"], 8, 128)
    unconv = lambda a: np.ascontiguousarray(a.transpose(0, 1, 4, 3, 2)).reshape(a.shape[0], a.shape[1], 2, 1024)
    c_p = unconv(cat("cp", 1)); c_s = unconv(cat("csm", 1))
    return (y_p, y_s, k_p, v_p, c_p, k_s, v_s, c_s)


FULL = dict(L=4, DFF=5504, NPS=2, SEQ=2048, NSS=2, TS=16, CACHE=512)


def kernel(**inputs):
    return run(inputs, FULL, 8)
```

```python
import contextlib
import numpy as np
import concourse.bass as bass
import concourse.mybir as mybir
from concourse.bass_utils import run_bass_kernel_spmd

F32 = mybir.dt.float32
BF16 = mybir.dt.bfloat16
AF = mybir.ActivationFunctionType
ALU = mybir.AluOpType
AX = mybir.AxisListType

D = 2048
KD = 16
DCONV = 1024
NREL = 257
EPS = 1e-6
NEG = -1e30
SCALE = 128 ** -0.5
NSLOT = 3
INPROJ = [x for c in range(8) for x in (("C", c), ("xc", c))] + [("B", c) for c in range(8)] + \
         [("q", c) for c in range(8)] + [("k", c) for c in range(8)]
COL0 = {"B": 0, "C": 1024, "xc": 2048, "q": 3072, "k": 4096}


class Trk:
    def __init__(self):
        self.ops = {e: [] for e in ("pe", "act", "dve", "pool", "sp")}
        self.cnt = {e: 0 for e in ("pe", "act", "dve", "pool")}
        self.tot = {}
        self.res = {}
        self.seen = {e: {} for e in self.ops}
        self.overlaps = {}
        self.pe_pending = False

    def _r(self, k):
        if k not in self.res:
            self.res[k] = {"w": None, "r": {}}
        return self.res[k]

    def op(self, eng, meth, reads=(), writes=(), inc=True, dma=None, **kw):
        fn = (meth, kw)
        own = eng if eng in self.cnt else None
        waits = {}

        def need(ev, skip_same):
            if ev is None:
                return
            s, v = ev
            if skip_same and s == own and dma is None:
                return
            if self.seen[eng].get(s, 0) >= v:
                return
            if waits.get(s, 0) < v:
                waits[s] = v

        for k in reads:
            need(self._r(k)["w"], False)
        for k in writes:
            for kk in [k] + self.overlaps.get(k, []):
                r = self._r(kk)
                need(r["w"], True)
                for s, v in r["r"].items():
                    need((s, v), True)
        for s, v in waits.items():
            self.seen[eng][s] = v
        rec = {"fn": fn, "waits": list(waits.items()), "inc": None}
        self.ops[eng].append(rec)
        if dma is not None:
            self.tot[dma] = self.tot.get(dma, 0) + 16
            ev = (dma, self.tot[dma])
            rec["inc"] = (dma, 16)
        elif eng == "pe":
            ev = ("pe", self.cnt["pe"] + 1)
            self.pe_pending = True
            if inc:
                self.flush_pe()
        else:
            self.cnt[eng] += 1
            ev = (eng, self.cnt[eng])
            rec["inc"] = (eng, 1)
        for k in writes:
            r = self._r(k)
            r["w"] = ev
            r["r"] = {}
        for k in reads:
            r = self._r(k)
            if r["r"].get(ev[0], 0) < ev[1]:
                r["r"][ev[0]] = ev[1]
        return ev

    def flush_pe(self):
        if self.pe_pending:
            self.ops["pe"][-1]["inc"] = ("pe", 1)
            self.cnt["pe"] += 1
            self.pe_pending = False


def build_program(cfg):
    L = cfg["L"]; DFF = cfg["DFF"]; NF = DFF // 128
    NPS = cfg["NPS"]; SEQ = cfg["SEQ"]; NSS = cfg["NSS"]; TS = cfg["TS"]; CACHE = cfg["CACHE"]
    NT = SEQ // 512
    DIN = 6144
    assert NSS == 2 and TS == 16 and CACHE == 512

    nc = bass.Bass("TRN2", target_bir_lowering=False)

    def din(name, shape, dt=F32):
        return nc.dram_tensor(name, list(shape), dt, kind="ExternalInput")

    def dout(name, shape):
        return nc.dram_tensor(name, list(shape), F32, kind="ExternalOutput")

    xp = din("xp", [NPS, SEQ, D]); xs_in = din("xs", [NSS, TS, D])
    ck = din("ck", [L, NSS, CACHE, 1024]); cv = din("cv", [L, NSS, CACHE, 1024])
    sconv = din("sconv", [L, NSS, 128, 8, 2])
    wgu = [din("wgu1", [L, NF, 128, 4096]), din("wgu2", [L, NF, 128, 4096])]
    wd = [din("wd1", [L, 4, 128, NF * 512]), din("wd2", [L, 4, 128, NF * 512])]
    w_inf = din("w_inf", [L, 20, 128, 4096]); w_int = din("w_int", [L, 2, 2, 128, 8192])
    w_out = din("w_out", [L, 4, 128, 8192])
    gpre_d = din("gpre", [128, L * 3 * 16]); gmix_d = din("gmix", [128, L * 16])
    convw_d = din("convw", [128, L * 24]); gpost_d = din("gpost", [L * 3, D])
    bias_d = din("biasx", [L, 128, 8, 640]); ident_d = din("ident", [128, 128])

    yp = dout("yp", [NPS, SEQ, D]); ys = dout("ys", [NSS, TS, D])
    kp = dout("kp", [L, NPS, 512, 1024]); vp = dout("vp", [L, NPS, 512, 1024])
    cp = dout("cp", [L, NPS, 128, 8, 2])
    ksm = dout("ksm", [L, NSS, TS, 1024]); vsm = dout("vsm", [L, NSS, TS, 1024])
    csm = dout("csm", [L, NSS, 128, 8, 2])
    kvs_k = nc.dram_tensor("kvs_k", [L, 128, 8, 512], BF16)
    kvs_v = nc.dram_tensor("kvs_v", [L, 128, 4, 1024], BF16)

    T = Trk()
    es = contextlib.ExitStack()

    def sb(name, shape, dt):
        return es.enter_context(nc.sbuf_tensor(name, list(shape), dt))

    xres = sb("xres", [128, 4, D], F32)
    xnT = sb("xnT", [128, KD, 512], BF16)
    NB = max(NF * 512, 4608 + 8704 + 10240 + 10240)
    regB = sb("regB", [128, NB], BF16)
    regA = sb("regA", [128, 8224], F32)
    wring = sb("wring", [128, NSLOT, 4096], BF16)
    gpost = sb("gpost_s", [128, D], F32)
    xsb = sb("xsb_s", [128, D], BF16)
    tmp = sb("tmp_s", [128, 4, 512], F32)
    Sb = sb("Sb", [128, 2, 640], F32)
    Pb = sb("Pb", [128, 2, 640], BF16)
    PT = sb("PT", [128, 2, 640], BF16)
    sqt = sb("sqt", [128, 2, 512], BF16)
    stat = sb("stat_s", [128, 64], F32)
    gpre = sb("gpre_s", [128, L * 48], F32)
    gmix = sb("gmix_s", [128, L * 16], F32)
    convw = sb("convw_s", [128, L * 24], F32)
    hist = sb("hist_s", [128, L * 32], F32)
    identf = sb("identf", [128, 128], F32)
    ident = sb("ident_s", [128, 128], BF16)
    ones = sb("ones_s", [128, 128], BF16)
    epsc = sb("epsc", [128, 1], F32)
    psb = [es.enter_context(nc.psum_tensor(f"ps{i}", [128, 512], F32)) for i in range(8)]

    hT = regB[:, 0:NF * 512].rearrange("p (j t) -> p j t", j=NF)
    QTp = regB[:, 0:4096].rearrange("p (h t) -> p h t", h=8)
    QTs = regB[:, 0:512].rearrange("p (h t) -> p h t", h=8)
    kstage = regB[:, 512:4608].rearrange("p (s e) -> p s e", s=4)
    KTcat = regB[:, 4608:4608 + 8704].rearrange("p (h k) -> p h k", h=8)
    Vcat = regB[:, 13312:13312 + 10240].rearrange("p (c e) -> p c e", c=10)
    biasv = regB[:, 23552:23552 + 10240].bitcast(F32).rearrange("p (h j) -> p h j", h=8)
    fbuf = regA[:, 0:8192].rearrange("p (s d) -> p s d", s=4)
    yc = regA[:, 4112:8208].rearrange("p (c t) -> p c t", c=8)
    ya = regA[:, 0:4096].rearrange("p (g e) -> p g e", g=4)

    mixkeys = ["QT", "KTcat", "Vcat", "bias", "kstage"]
    T.overlaps["hT"] = list(mixkeys)
    for k in mixkeys:
        T.overlaps[k] = ["hT"]
    T.overlaps["QT"].append("kstage"); T.overlaps["kstage"].append("QT")
    akeys = [f"gx{c}" for c in range(8)] + [f"yc{c}" for c in range(8)] + [f"ya{g}" for g in range(4)]
    for s in range(4):
        T.overlaps[f"fbuf{s}"] = list(akeys)
    for k in akeys:
        T.overlaps[k] = [f"fbuf{s}" for s in range(4)]
    for g in range(4):
        T.overlaps[f"ya{g}"] += [f"gx{c}" for c in range(8)]
    for c in range(8):
        T.overlaps[f"gx{c}"] += [f"ya{g}" for g in range(4)]

    st_i = [0]

    def statcol():
        st_i[0] = (st_i[0] + 1) % 60
        return st_i[0], f"st{st_i[0]}"

    bank_i = [0]

    def bank():
        bank_i[0] = (bank_i[0] + 1) % 8
        return psb[bank_i[0]], f"ps{bank_i[0]}"

    tmp_i = [0]

    def gettmp():
        tmp_i[0] = (tmp_i[0] + 1) % 4
        return tmp[:, tmp_i[0], :], f"tmp{tmp_i[0]}"

    rot = {"S": 0, "P": 0, "PT": 0, "sq": 0}

    def rot2(k):
        rot[k] ^= 1
        return rot[k]

    def dma_sp(out, in_, sem, reads, writes):
        T.op("sp", "dma_start", out=out, in_=in_, reads=reads, writes=writes, dma=sem)

    def dma_pool(out, in_, sem, reads, writes):
        T.op("pool", "dma_start", out=out, in_=in_, reads=reads, writes=writes, dma=sem)

    wq = []
    wstate = {"issued": 0, "used": 0}

    def w_issue_upto(n):
        while wstate["issued"] < min(n, len(wq)):
            u = wstate["issued"]
            slot = u % NSLOT
            for (dst, src) in wq[u]:
                dma_pool(dst(slot), src, f"w{slot}", reads=(), writes=(f"ws{slot}",))
            wstate["issued"] += 1

    def w_next():
        u = wstate["used"]
        w_issue_upto(u + NSLOT)
        wstate["used"] += 1
        slot = u % NSLOT
        return slot, f"ws{slot}"

    def plan_unit(src_ap, n):
        wq.append([(lambda slot: wring[:, slot, 0:n], src_ap)])

    STOP = cfg.get("stop")

    def plan_tile_layer(l, need_k):
        for f in range(2):
            if f == 1 and STOP == "ffn1":
                return
            if f == 1:
                for i in range(len(INPROJ) // 2):
                    plan_unit(w_inf[l, i], 4096)
                if STOP == "mixA":
                    return
                for sec in ([0] + ([1] if need_k else [])):
                    for c2 in range(2):
                        for u in range(2):
                            plan_unit(w_int[l, sec, c2][:, u * 4096:(u + 1) * 4096], 4096)
                if STOP in ("mixB", "mixBn", "mixC", "mixD"):
                    return
                for c in range(4):
                    for u in range(2):
                        plan_unit(w_out[l, c][:, u * 4096:(u + 1) * 4096], 4096)
                if STOP == "mix":
                    return
            for j in range(NF):
                plan_unit(wgu[f][l, j], 4096)
            for c in range(4):
                for u in range((NF + 7) // 8):
                    nk = min(8, NF - 8 * u)
                    plan_unit(wd[f][l, c][:, 8 * u * 512:(8 * u + nk) * 512], nk * 512)

    def rstd_from(ss_c, ss_k, n, rows):
        c1, k1 = statcol()
        T.op("act", "activation", out=stat[:rows, c1:c1 + 1], in_=stat[:rows, ss_c:ss_c + 1], func=AF.Sqrt,
             bias=epsc[:rows, 0:1], scale=1.0 / n, reads=(ss_k, "epsc"), writes=(k1,))
        c2, k2 = statcol()
        T.op("dve", "reciprocal", out=stat[:rows, c2:c2 + 1], in_=stat[:rows, c1:c1 + 1],
             reads=(k1,), writes=(k2,))
        return c2, k2

    def sumsq(src_ap, src_keys, rows, junk_ap, junk_keys):
        c0, k0 = statcol()
        T.op("dve", "memset", ap=stat[:rows, c0:c0 + 1], constant=0.0, writes=(k0,))
        T.op("act", "activation", out=junk_ap, in_=src_ap, func=AF.Square, accum_out=stat[:rows, c0:c0 + 1],
             reads=tuple(src_keys) + (k0,), writes=tuple(junk_keys) + (k0,))
        return c0, k0

    def transpose_to_xnT(half, rows, nchunk, kbase, tok0, gain_ap):
        pb_, pk = bank()
        pv = pb_[:, :].bitcast(BF16).rearrange("p (k t) -> p k t", k=8)
        xk = f"xsb{half}"
        for kk in range(nchunk):
            c0 = half * 1024 + kk * 128
            T.op("pe", "transpose", out=pv[:, kk, :rows], in_=xsb[:rows, c0:c0 + 128], identity=ident[:rows, :rows],
                 reads=(xk, "ident"), writes=(pk,), inc=(kk == nchunk - 1))
        g_b = gain_ap.unsqueeze(2).to_broadcast([128, nchunk, rows])
        T.op("dve", "tensor_tensor", out=xnT[:, kbase:kbase + nchunk, tok0:tok0 + rows],
             in0=pv[:, 0:nchunk, :rows], in1=g_b, op=ALU.mult, reads=(pk, "consts"), writes=("xnT",))

    def prenorm(tl, l, gi):
        rows = tl["PS"]
        rs = []
        for s in range(tl["NS"]):
            c0, k0 = sumsq(xres[:rows, s, :], (f"xres{s}",), rows, fbuf[:rows, s, :], (f"fbuf{s}",))
            rs.append(rstd_from(c0, k0, D, rows))
        for s in range(tl["NS"]):
            c2, k2 = rs[s]
            for half in range(2):
                if half == 0:
                    T.op("act", "activation", out=xsb[:rows, 0:1024], in_=xres[:rows, s, 0:1024], func=AF.Copy,
                         scale=stat[:rows, c2:c2 + 1], reads=(f"xres{s}", k2), writes=("xsb0",))
                else:
                    T.op("dve", "tensor_scalar", out=xsb[:rows, 1024:2048], in0=xres[:rows, s, 1024:2048],
                         scalar1=stat[:rows, c2:c2 + 1], scalar2=None, op0=ALU.mult,
                         reads=(f"xres{s}", k2), writes=("xsb1",))
                gb = (l * 3 + gi) * 16 + half * 8
                transpose_to_xnT(half, rows, 8, half * 8, s * 128, gpre[:, gb:gb + 8])

    def load_gpost(l, gi):
        dma_sp(gpost[:, :], gpost_d[l * 3 + gi:l * 3 + gi + 1, :].partition_broadcast(128), "gp", (), ("gpost",))

    def postnorm(tl, coef=0.5):
        rows = tl["PS"]
        for s in range(tl["NS"]):
            c0, k0 = sumsq(fbuf[:rows, s, :], (f"fbuf{s}",), rows, xsb[:rows, :], ("xsb0", "xsb1"))
            c2, k2 = rstd_from(c0, k0, D, rows)
            T.op("dve", "scalar_tensor_tensor", out=fbuf[:rows, s, :], in0=fbuf[:rows, s, :],
                                                         scalar=stat[:rows, c2:c2 + 1], in1=gpost[:rows, :],
                                                         op0=ALU.mult, op1=ALU.mult,
                 reads=(f"fbuf{s}", k2, "gpost"), writes=(f"fbuf{s}",))
            T.op("dve", "scalar_tensor_tensor", out=xres[:rows, s, :], in0=fbuf[:rows, s, :],
                                                         scalar=coef, in1=xres[:rows, s, :],
                                                         op0=ALU.mult, op1=ALU.add,
                 reads=(f"fbuf{s}", f"xres{s}"), writes=(f"xres{s}",))

    evac_i = [0]

    def evac_copy(out, in_, reads, writes):
        evac_i[0] ^= 1
        if evac_i[0]:
            T.op("act", "copy", out=out, in_=in_, reads=reads, writes=writes)
        else:
            T.op("dve", "tensor_copy", out=out, in_=in_, reads=reads, writes=writes)

    def slab_matmul(tl, lhs_fn, lhs_key, nktot, col_slabs):
        rows = tl["PS"]
        for c in range(col_slabs):
            accs = [bank() for _ in range(tl["NS"])]
            k = 0
            while k < nktot:
                nk = min(8, nktot - k)
                slot, wk = w_next()
                wv = wring[:, slot, 0:nk * 512].rearrange("p (k c) -> p k c", k=nk)
                for kk in range(nk):
                    for s in range(tl["NS"]):
                        kq = k + kk
                        last = (kq == nktot - 1)
                        T.op("pe", "matmul", out=accs[s][0][:rows, :], lhsT=lhs_fn(kq, s), rhs=wv[:, kk, :],
                            start=(kq == 0), stop=last,
                             reads=(lhs_key, wk), writes=(accs[s][1],), inc=last)
                T.flush_pe()
                k += nk
            for s in range(tl["NS"]):
                evac_copy(fbuf[:rows, s, c * 512:(c + 1) * 512], accs[s][0][:rows, :],
                          (accs[s][1],), (f"fbuf{s}",))

    def ffn(tl, l, f):
        Tn = tl["T"]
        load_gpost(l, 0 if f == 0 else 2)
        prenorm(tl, l, 0 if f == 0 else 2)
        for j in range(NF):
            slot, wk = w_next()
            wv = wring[:, slot, :].rearrange("p (g k c) -> p g k c", g=2, k=16)
            pg, pgk = bank(); pu, puk = bank()
            for g, (pp, ppk) in enumerate(((pg, pgk), (pu, puk))):
                for k in range(KD):
                    T.op("pe", "matmul", out=pp[:, :Tn], lhsT=wv[:, g, k, :], rhs=xnT[:, k, :Tn],
                                                                 start=(k == 0), stop=(k == KD - 1),
                         reads=("xnT", wk), writes=(ppk,), inc=(k == KD - 1))
            tv, tk = gettmp()
            T.op("act", "activation", out=tv[:, :Tn], in_=pg[:, :Tn], func=AF.Silu,
                 reads=(pgk,), writes=(tk,))
            T.op("dve", "tensor_tensor", out=hT[:, j, :Tn], in0=pu[:, :Tn], in1=tv[:, :Tn], op=ALU.mult,
                 reads=(puk, tk), writes=("hT",))
        slab_matmul(tl, lambda kq, s: hT[:, kq, s * 128:s * 128 + tl["PS"]], "hT", NF, 4)
        postnorm(tl)

    def mix(tl, l):
        Tn = tl["T"]; nseq = tl["nseq"]; Ls = tl["Lseg"]; samp = tl["samp"]
        QT = QTs if samp else QTp
        Wg = 2 + Ls
        gx = regA[:, 0:8 * nseq * Wg].rearrange("p (c a t) -> p c a t", c=8, a=nseq)
        load_gpost(l, 1)
        prenorm(tl, l, 1)
        dma_sp(biasv, bias_d[l], "bias", (), ("bias",))
        T.op("dve", "memset", ap=biasv[0:64, :, 576:640], constant=NEG, writes=("bias",))
        T.op("dve", "memset", ap=biasv[64:128, :, 0:64], constant=NEG, writes=("bias",))
        if samp:
            T.op("dve", "memset", ap=biasv[0:32, :, 528:544], constant=NEG, writes=("bias",))
        hb = l * 32
        hv = hist[:, hb:hb + 32].rearrange("p (a c r) -> p a c r", a=2, c=8)
        if samp:
            for a in range(nseq):
                dma_pool(Vcat[:, 5 * a:5 * a + 4, :], cv[l, a].rearrange("(s p) e -> p s e", p=128), "cv", (), ("Vcat",))
                dma_pool(kstage, ck[l, a].rearrange("(s p) e -> p s e", p=128), "ck", (), ("kstage",))
                for h in range(8):
                    if h % 2 == 0:
                        pb_, pk = bank()
                        pv = pb_[:, :].bitcast(BF16).rearrange("p (k t) -> p k t", k=8)
                    for s4 in range(4):
                        T.op("pe", "transpose", out=pv[:, (h % 2) * 4 + s4, :], in_=kstage[:, s4, h * 128:(h + 1) * 128], identity=ident[:, :],
                             reads=("kstage", "ident"), writes=(pk,), inc=(h % 2 == 1 and s4 == 3))
                    if h % 2 == 1:
                        evac_copy(KTcat[:, h - 1:h + 1, a * 544:a * 544 + 512],
                                  pv[:, :, :].rearrange("p (h s) t -> p h (s t)", h=2), (pk,), ("KTcat",))
                dma_sp(hv[:, a, :, :], sconv[l, a], "hist", (), ("hist",))
        elif tl["ti"] > 0:
            dma_sp(KTcat[:, :, 0:512], kvs_k[l], "kvlk", (f"kvsk{l}",), ("KTcat",))
            dma_sp(Vcat[:, 0:4, :], kvs_v[l], "kvlv", (f"kvsv{l}",), ("Vcat",))
        else:
            T.op("dve", "memset", ap=hist[:, hb:hb + 32], constant=0.0, writes=("hist",))
        cwv = convw[:, l * 24:(l + 1) * 24].rearrange("p (c j) -> p c j", c=8)
        ctmp = {}
        for i, (kind, c) in enumerate(INPROJ):
            if i % 2 == 0:
                slot, wk = w_next()
                wv = wring[:, slot, :].rearrange("p (g k c) -> p g k c", g=2, k=16)
            g = i % 2
            pp, ppk = bank()
            for k in range(KD):
                T.op("pe", "matmul", out=pp[:, :Tn], lhsT=wv[:, g, k, :], rhs=xnT[:, k, :Tn],
                                                                    start=(k == 0), stop=(k == KD - 1),
                     reads=("xnT", wk), writes=(ppk,), inc=(k == KD - 1))
            if kind == "C":
                tv, tk = gettmp()
                ctmp[c] = (tv, tk)
                T.op("act", "copy", out=tv[:, :Tn], in_=pp[:, :Tn], reads=(ppk,), writes=(tk,))
            elif kind == "xc":
                tv, tk = ctmp[c]
                for a in range(nseq):
                    T.op("act", "copy", out=gx[:, c, a, 0:2], in_=hv[:, a, c, :],
                         reads=("hist",), writes=(f"gx{c}",))
                    T.op("dve", "tensor_tensor", out=gx[:, c, a, 2:2 + Ls], in0=pp[:, a * Ls:(a + 1) * Ls], in1=tv[:, a * Ls:(a + 1) * Ls], op=ALU.mult,
                         reads=(ppk, tk), writes=(f"gx{c}",))
                    ycv = yc[:, c, a * Ls:(a + 1) * Ls]
                    T.op("dve", "tensor_scalar", out=ycv, in0=gx[:, c, a, 0:Ls], scalar1=cwv[:, c, 0:1], scalar2=None, op0=ALU.mult,
                         reads=(f"gx{c}", "consts"), writes=(f"yc{c}",))
                    for jj in (1, 2):
                        T.op("dve", "scalar_tensor_tensor", out=ycv, in0=gx[:, c, a, jj:jj + Ls], scalar=cwv[:, c, jj:jj + 1], in1=ycv,
                            op0=ALU.mult, op1=ALU.add,
                             reads=(f"gx{c}", f"yc{c}", "consts"), writes=(f"yc{c}",))
                if c == 7:
                    for a in range(nseq):
                        for cc in range(8):
                            T.op("act", "copy", out=hv[:, a, cc, :], in_=gx[:, cc, a, tl["Lreal"]:tl["Lreal"] + 2],
                                 reads=(f"gx{cc}",), writes=("hist",))
                    if samp:
                        for a in range(nseq):
                            dma_sp(csm[l, a], hv[:, a, :, :], "cst", ("hist",), ())
                    elif tl["last"]:
                        dma_sp(cp[l, tl["b"]], hv[:, 0, :, :], "cst", ("hist",), ())
            elif kind == "B":
                T.op("dve", "tensor_tensor", out=yc[:, c, :Tn], in0=pp[:, :Tn], in1=yc[:, c, :Tn], op=ALU.mult,
                     reads=(ppk, f"yc{c}"), writes=(f"yc{c}",))
            elif kind == "q":
                T.op("act", "copy", out=QT[:, c, :Tn], in_=pp[:, :Tn], reads=(ppk,), writes=("QT",))
            else:
                if samp:
                    for a in range(nseq):
                        T.op("act", "copy", out=KTcat[:, c, a * 544 + 512:a * 544 + 544],
                                                                      in_=pp[:, a * Ls:(a + 1) * Ls],
                             reads=(ppk,), writes=("KTcat",))
                else:
                    T.op("act", "copy", out=KTcat[:, c, 512:1024], in_=pp[:, :Tn],
                         reads=(ppk,), writes=("KTcat",))
        if STOP == "mixA":
            return
        groups = tl["groups"]
        need_out = samp or tl["last"]
        for sec in (["v"] + (["k"] if need_out else [])):
            for c2 in range(2):
                accs = [bank() for _ in groups]
                for u in range(2):
                    slot, wk = w_next()
                    wv = wring[:, slot, :].rearrange("p (k c) -> p k c", k=8)
                    for kk in range(8):
                        kq = 8 * u + kk
                        for gi, (t0, nr) in enumerate(groups):
                            T.op("pe", "matmul", out=accs[gi][0][:nr, :], lhsT=xnT[:, kq, t0:t0 + nr], rhs=wv[:, kk, :],
                                start=(kq == 0), stop=(kq == KD - 1),
                                 reads=("xnT", wk), writes=(accs[gi][1],), inc=(kq == KD - 1))
                    T.flush_pe()
                for gi, (t0, nr) in enumerate(groups):
                    pa, pak = accs[gi]
                    vch = (5 * gi + 4) if samp else (4 + gi)
                    outp = need_out and STOP != "mixBn"
                    if outp:
                        tv, tk = gettmp()
                        T.op("act", "copy", out=tv[:nr, :], in_=pa[:nr, :], reads=(pak,), writes=(tk,))
                        if sec == "v":
                            T.op("dve", "tensor_copy", out=Vcat[:nr, vch, c2 * 512:(c2 + 1) * 512], in_=tv[:nr, :],
                                 reads=(tk,), writes=("Vcat",))
                    elif sec == "v":
                        T.op("act", "copy", out=Vcat[:nr, vch, c2 * 512:(c2 + 1) * 512], in_=pa[:nr, :],
                             reads=(pak,), writes=("Vcat",))
                    if outp:
                        if samp:
                            nr = TS
                            dst = (vsm if sec == "v" else ksm)[l, gi, :, c2 * 512:(c2 + 1) * 512]
                        else:
                            dst = (vp if sec == "v" else kp)[l, tl["b"], t0:t0 + nr, c2 * 512:(c2 + 1) * 512]
                        dma_sp(dst, tv[:nr, :], "ost", (tk,), ("ostq",))
        if (not samp) and (not tl["last"]):
            dma_sp(kvs_k[l], KTcat[:, :, 512:1024], "kvstk", ("KTcat",), (f"kvsk{l}",))
            dma_sp(kvs_v[l], Vcat[:, 4:8, :], "kvstv", ("Vcat",), (f"kvsv{l}",))
        if STOP in ("mixB", "mixBn"):
            return
        def conv_norm():
            pn, pnk = bank()
            for c in range(8):
                r = rot2("sq")
                T.op("act", "activation", out=sqt[:, r, :Tn], in_=yc[:, c, :Tn], func=AF.Square,
                     reads=(f"yc{c}",), writes=(f"sq{r}",))
                T.op("pe", "matmul", out=pn[:, :Tn], lhsT=ones[:, :], rhs=sqt[:, r, :Tn], start=(c == 0), stop=(c == 7),
                     reads=(f"sq{r}", "ones"), writes=(pnk,), inc=True)
            rv, rk = gettmp()
            T.op("act", "activation", out=rv[:, :Tn], in_=pn[:, :Tn], func=AF.Sqrt, bias=epsc[:, 0:1], scale=1.0 / DCONV,
                 reads=(pnk, "epsc"), writes=(rk,))
            T.op("dve", "reciprocal", out=rv[:, :Tn], in_=rv[:, :Tn], reads=(rk,), writes=(rk,))
            for c in range(8):
                T.op("dve", "scalar_tensor_tensor", out=xnT[:, c, :Tn], in0=yc[:, c, :Tn],
                                                                scalar=gmix[:, l * 16 + c:l * 16 + c + 1], in1=rv[:, :Tn],
                                                                op0=ALU.mult, op1=ALU.mult,
                     reads=(f"yc{c}", rk, "consts"), writes=("xnT",))
        if STOP == "mixC":
            return
        items = []
        for gi, (t0, nq) in enumerate(groups):
            if samp:
                ks, ke, jo = 544 * gi, 544 * gi + 544, 0
            else:
                ks = 512 if tl["ti"] == 0 else 128 * gi
                ke = 128 * gi + 640
                jo = ks - 128 * gi
            for h in range(8):
                items.append(dict(gi=gi, t0=t0, nq=nq, ks=ks, ke=ke, jo=jo, h=h, nk=ke - ks))

        def st1(i):
            it = items[i]; nq = it["nq"]; nk = it["nk"]; h = it["h"]; ks = it["ks"]; ke = it["ke"]; jo = it["jo"]; t0 = it["t0"]
            n1 = min(512, nk); n2 = nk - n1
            r = i % 2
            b1, b1k = bank()
            T.op("pe", "matmul", out=b1[:nq, :n1], lhsT=QT[:, h, t0:t0 + nq], rhs=KTcat[:, h, ks:ks + n1],
                 start=True, stop=True, reads=("QT", "KTcat"), writes=(b1k,), inc=True)
            if n2:
                b2, b2k = bank()
                T.op("pe", "matmul", out=b2[:nq, :n2], lhsT=QT[:, h, t0:t0 + nq], rhs=KTcat[:, h, ks + n1:ke],
                     start=True, stop=True, reads=("QT", "KTcat"), writes=(b2k,), inc=True)
            T.op("dve", "scalar_tensor_tensor", out=Sb[:nq, r, 0:n1], in0=b1[:nq, :n1], scalar=SCALE,
                 in1=biasv[:nq, h, jo:jo + n1], op0=ALU.mult, op1=ALU.add, reads=(b1k, "bias"), writes=(f"S{r}",))
            if n2:
                T.op("dve", "scalar_tensor_tensor", out=Sb[:nq, r, n1:nk], in0=b2[:nq, :n2], scalar=SCALE,
                     in1=biasv[:nq, h, jo + n1:jo + nk], op0=ALU.mult, op1=ALU.add, reads=(b2k, "bias"), writes=(f"S{r}",))
            cm, km = statcol()
            T.op("dve", "reduce_max", out=stat[:nq, cm:cm + 1], in_=Sb[:nq, r, 0:nk], axis=AX.X,
                 reads=(f"S{r}",), writes=(km,))
            cn, kn = statcol()
            T.op("dve", "tensor_scalar", out=stat[:nq, cn:cn + 1], in0=stat[:nq, cm:cm + 1],
                 scalar1=-1.0, scalar2=None, op0=ALU.mult, reads=(km,), writes=(kn,))
            csu, ksu = statcol()
            T.op("dve", "memset", ap=stat[:nq, csu:csu + 1], constant=0.0, writes=(ksu,))
            T.op("act", "activation", out=Pb[:nq, r, 0:nk], in_=Sb[:nq, r, 0:nk], func=AF.Exp,
                 bias=stat[:nq, cn:cn + 1], scale=1.0, accum_out=stat[:nq, csu:csu + 1],
                 reads=(f"S{r}", kn, ksu), writes=(f"P{r}", ksu))
            it["csu"] = csu; it["ksu"] = ksu

        def st2(i):
            it = items[i]; nq = it["nq"]; nk = it["nk"]
            r = i % 2
            nch = (nk + 127) // 128
            pb_, pk = bank()
            pv = pb_[:, :].bitcast(BF16).rearrange("p (k t) -> p k t", k=8)
            for ci in range(nch):
                csz = min(128, nk - ci * 128)
                T.op("pe", "transpose", out=pv[:csz, ci, :nq], in_=Pb[:nq, r, ci * 128:ci * 128 + csz],
                     identity=ident[:nq, :nq], reads=(f"P{r}", "ident"), writes=(pk,), inc=(ci == nch - 1))
            PTv = PT[:, r, :].rearrange("p (k t) -> p k t", k=5)
            nfull = nk // 128
            T.op("act", "copy", out=PTv[:, 0:nfull, :nq], in_=pv[:, 0:nfull, :nq], reads=(pk,), writes=(f"PT{r}",))
            if nfull < nch:
                cs = nk - nfull * 128
                T.op("act", "copy", out=PTv[:cs, nfull, :nq], in_=pv[:cs, nfull, :nq], reads=(pk,), writes=(f"PT{r}",))

        def st3(i):
            it = items[i]; nq = it["nq"]; nk = it["nk"]; h = it["h"]; gi = it["gi"]; ks = it["ks"]; t0 = it["t0"]
            r = i % 2
            nch = (nk + 127) // 128
            PTv = PT[:, r, :].rearrange("p (k t) -> p k t", k=5)
            po, pok = bank()
            for ci in range(nch):
                csz = min(128, nk - ci * 128)
                vch = (5 * gi + ci) if samp else (ks // 128 + ci)
                T.op("pe", "matmul", out=po[:nq, 0:128], lhsT=PTv[:csz, ci, :nq], rhs=Vcat[:csz, vch, h * 128:(h + 1) * 128],
                     start=(ci == 0), stop=(ci == nch - 1), reads=(f"PT{r}", "Vcat"), writes=(pok,), inc=(ci == nch - 1))
            cr, kr = statcol()
            T.op("dve", "reciprocal", out=stat[:nq, cr:cr + 1], in_=stat[:nq, it["csu"]:it["csu"] + 1],
                 reads=(it["ksu"],), writes=(kr,))
            T.op("dve", "tensor_scalar", out=ya[:nq, gi, h * 128:(h + 1) * 128], in0=po[:nq, 0:128],
                 scalar1=stat[:nq, cr:cr + 1], scalar2=None, op0=ALU.mult, reads=(pok, kr), writes=(f"ya{gi}",))
            if h == 7:
                pending.append((cur_step[0] + 3, gi, nq, t0))

        def ya_norm(gi, nq, t0):
            c0, k0 = sumsq(ya[:nq, gi, :], (f"ya{gi}",), nq, xsb[:nq, 1024:2048], ("xsb1",))
            c2, k2 = rstd_from(c0, k0, 1024, nq)
            T.op("dve", "tensor_scalar", out=xsb[:nq, 0:1024], in0=ya[:nq, gi, :],
                 scalar1=stat[:nq, c2:c2 + 1], scalar2=None, op0=ALU.mult, reads=(f"ya{gi}", k2), writes=("xsb0",))
            transpose_to_xnT(0, nq, 8, 8, t0, gmix[:, l * 16 + 8:l * 16 + 16])

        nit = len(items)
        pending = []
        cur_step = [0]
        for i in range(nit + 2):
            cur_step[0] = i
            if i < nit:
                st1(i)
            if 0 <= i - 1 < nit:
                st2(i - 1)
            if 0 <= i - 2 < nit:
                st3(i - 2)
            if i == 4:
                conv_norm()
            while pending and pending[0][0] <= i:
                _, g_, nq_, t0_ = pending.pop(0)
                ya_norm(g_, nq_, t0_)
        for _, g_, nq_, t0_ in pending:
            ya_norm(g_, nq_, t0_)
        if STOP == "mixD":
            return
        slab_matmul(tl, lambda kq, s: xnT[:, kq, s * 128:s * 128 + tl["PS"]], "xnT", KD, 4)
        postnorm(tl, 1.0)

    tiles = []
    for b in range(NPS):
        for ti in range(NT):
            tiles.append(dict(samp=False, b=b, ti=ti, last=(ti == NT - 1), T=512, NS=4, PS=128, nseq=1, Lseg=512, Lreal=512,
                              groups=[(i * 128, 128) for i in range(4)]))
    tiles.append(dict(samp=True, b=0, ti=0, last=True, T=64, NS=1, PS=64, nseq=NSS, Lseg=32, Lreal=TS,
                      groups=[(a * 32, 32) for a in range(NSS)]))
    for tl in tiles:
        for l in range(L):
            plan_tile_layer(l, tl["samp"] or tl["last"])

    dma_sp(identf[:, :], ident_d[:, :], "cst0", (), ("consts",))
    dma_sp(gpre[:, :], gpre_d[:, :], "cst0", (), ("consts",))
    dma_sp(gmix[:, :], gmix_d[:, :], "cst0", (), ("consts",))
    dma_sp(convw[:, :], convw_d[:, :], "cst0", (), ("consts",))
    T.op("dve", "tensor_copy", out=ident[:, :], in_=identf[:, :], reads=("consts",), writes=("ident",))
    T.op("dve", "memset", ap=ones[:, :], constant=1.0, writes=("ones",))
    T.op("dve", "memset", ap=epsc[:, :], constant=EPS, writes=("epsc",))

    for tl in tiles:
        xk = tuple(f"xres{s}" for s in range(tl["NS"]))
        if tl["samp"]:
            T.op("dve", "memset", ap=xres[:64, 0, :], constant=0.0, writes=xk)
            for a in range(NSS):
                dma_sp(xres[32 * a:32 * a + TS, 0, :], xs_in[a], "ldx", (), xk)
        else:
            t0 = tl["ti"] * 512
            dma_sp(xres[:, :, :], xp[tl["b"], t0:t0 + 512, :].rearrange("(s p) d -> p s d", p=128), "ldx", (), xk)
        for l in range(L):
            ffn(tl, l, 0)
            if STOP != "ffn1":
                mix(tl, l)
                if STOP is None:
                    ffn(tl, l, 1)
        if tl["samp"]:
            for a in range(NSS):
                dma_sp(ys[a], xres[32 * a:32 * a + TS, 0, :], "sty", xk, ())
        else:
            t0 = tl["ti"] * 512
            dma_sp(yp[tl["b"], t0:t0 + 512, :].rearrange("(s p) d -> p s d", p=128), xres[:, :, :], "sty", xk, ())
    assert wstate["used"] == len(wq), (wstate, len(wq))
    T.flush_pe()

    semnames = set(T.cnt) | set(T.tot)
    sems = {n: es.enter_context(nc.semaphore("s_" + n)) for n in sorted(semnames)}
    engs = {"pe": "tensor", "act": "scalar", "dve": "vector", "pool": "gpsimd", "sp": "sync"}
    with nc.Block() as block:
        for en, attr in engs.items():
            def body(e, en=en):
                for rec in T.ops[en]:
                    for s, v in rec["waits"]:
                        e.wait_ge(sems[s], v)
                    ins = getattr(e, rec["fn"][0])(**rec["fn"][1])
                    if rec["inc"] is not None:
                        ins.then_inc(sems[rec["inc"][0]], rec["inc"][1])
                if en == "sp":
                    for s, v in T.tot.items():
                        e.wait_ge(sems[s], v)
            getattr(block, attr)(body)
    es.close()
    return nc


def host_inputs(inp, cfg, core):
    L = cfg["L"]; NPS = cfg["NPS"]; NSS = cfg["NSS"]
    f = lambda a: np.ascontiguousarray(np.asarray(a, dtype=np.float32))
    m = {}
    m["xp"] = f(inp["x_prompt"][core * NPS:(core + 1) * NPS])
    m["xs"] = f(inp["x_sample"][core * NSS:(core + 1) * NSS])
    m["ck"] = f(np.asarray(inp["cache_k"])[:, core * NSS:(core + 1) * NSS].reshape(L, NSS, 512, 1024))
    m["cv"] = f(np.asarray(inp["cache_v"])[:, core * NSS:(core + 1) * NSS].reshape(L, NSS, 512, 1024))
    sc = np.asarray(inp["state_conv"])[:, core * NSS:(core + 1) * NSS]
    m["sconv"] = f(sc.reshape(L, NSS, 2, 8, 128).transpose(0, 1, 4, 3, 2))
    return m


def shared_inputs(inp, cfg):
    L = cfg["L"]; NF = cfg["DFF"] // 128
    f = lambda a: np.ascontiguousarray(np.asarray(a, dtype=np.float32))
    A = lambda k: np.asarray(inp[k], dtype=np.float32)
    m = {}

    def gu(wg_, wu_):
        a = wg_.reshape(L, 16, 128, NF, 128).transpose(0, 3, 2, 1, 4)
        b = wu_.reshape(L, 16, 128, NF, 128).transpose(0, 3, 2, 1, 4)
        return np.stack([a, b], axis=3).reshape(L, NF, 128, 4096)

    def slabs(w, nk):
        return f(w.reshape(L, nk, 128, 4, 512).transpose(0, 3, 2, 1, 4)).reshape(L, 4, 128, nk * 512)

    m["wgu1"] = gu(A("ffn1_w_gate"), A("ffn1_w_up")); m["wgu2"] = gu(A("ffn2_w_gate"), A("ffn2_w_up"))
    m["wd1"] = slabs(A("ffn1_w_down"), NF); m["wd2"] = slabs(A("ffn2_w_down"), NF)
    m["w_out"] = slabs(A("w_out"), 16)
    wi = A("w_in")
    a = wi[:, :, 0:5120].reshape(L, 16, 128, 40, 128).transpose(0, 3, 2, 1, 4)
    order = [COL0[kind] // 128 + c for kind, c in INPROJ]
    a = a[:, order]
    m["w_inf"] = f(a.reshape(L, 20, 2, 128, 16, 128).transpose(0, 1, 3, 2, 4, 5)).reshape(L, 20, 128, 4096)
    secs = [wi[:, :, s0:s0 + 1024].reshape(L, 16, 128, 2, 512).transpose(0, 3, 2, 1, 4) for s0 in (5120, 4096)]
    m["w_int"] = np.stack(secs, axis=1).reshape(L, 2, 2, 128, 8192)
    pre = np.stack([np.asarray(inp["ln_ffn1_pre"]), np.asarray(inp["ln_mix_pre"]), np.asarray(inp["ln_ffn2_pre"])], 1)
    m["gpre"] = f(pre.reshape(L, 3, 16, 128).transpose(3, 0, 1, 2).reshape(128, L * 48))
    gm = np.concatenate([np.asarray(inp["g_conv_out"]), np.asarray(inp["g_attn_out"])], 1)
    m["gmix"] = f(gm.reshape(L, 16, 128).transpose(2, 0, 1).reshape(128, L * 16))
    cw = np.asarray(inp["conv_w"])
    m["convw"] = f(cw.reshape(L, 3, 8, 128).transpose(3, 0, 2, 1).reshape(128, L * 24))
    post = np.stack([np.asarray(inp["ln_ffn1_post"]), np.asarray(inp["ln_mix_post"]), np.asarray(inp["ln_ffn2_post"])], 1)
    m["gpost"] = f(post.reshape(L * 3, D))
    q = np.arange(128)[:, None]; j = np.arange(640)[None, :]
    idx = np.clip(q + 512 - j, -128, 128) + 128
    rb = np.asarray(inp["rel_bias"])
    m["biasx"] = f(rb[:, idx, :].transpose(0, 1, 3, 2))
    m["ident"] = np.eye(128, dtype=np.float32)
    return m


_CACHE = {}


def run(inp, cfg, n_cores):
    key = tuple(sorted((k, str(v)) for k, v in cfg.items()))
    if key not in _CACHE:
        _CACHE[key] = build_program(cfg)
    nc = _CACHE[key]
    sh = shared_inputs(inp, cfg)
    in_maps = []
    for c in range(n_cores):
        m = dict(sh)
        m.update(host_inputs(inp, cfg, c))
        in_maps.append(m)
    res = run_bass_kernel_spmd(nc, in_maps, core_ids=list(range(n_cores)))
    R = res.results
    L = cfg["L"]
    cat = lambda k, ax: np.concatenate([r[k] for r in R], axis=ax)
    y_p = cat("yp", 0); y_s = cat("ys", 0)
    k_p = cat("kp", 1).reshape(L, -1, 512, 8, 128); v_p = cat("vp", 1).reshape(L, -1, 512, 8, 128)
    k_s = cat("ksm", 1).reshape(L, -1, cfg["TS"], 8, 128); v_s = cat("vsm", 1).reshape(L, -1, cfg["TS"], 8, 128)
    unconv = lambda a: np.ascontiguousarray(a.transpose(0, 1, 4, 3, 2)).reshape(a.shape[0], a.shape[1], 2, 1024)
    c_p = unconv(cat("cp", 1)); c_s = unconv(cat("csm", 1))
    return (y_p, y_s, k_p, v_p, c_p, k_s, v_s, c_s)


FULL = dict(L=4, DFF=5504, NPS=2, SEQ=2048, NSS=2, TS=16, CACHE=512)


def kernel(**inputs):
    return run(inputs, FULL, 8)
```

```python
import contextlib
import numpy as np
import concourse.bass as bass
import concourse.mybir as mybir
from concourse.bass_utils import run_bass_kernel_spmd

F32 = mybir.dt.float32
BF16 = mybir.dt.bfloat16
AF = mybir.ActivationFunctionType
ALU = mybir.AluOpType
AX = mybir.AxisListType

D = 2048
KD = 16
DCONV = 1024
NREL = 257
EPS = 1e-6
NEG = -1e30
SCALE = 128 ** -0.5
NSLOT = 3
INPROJ = [x for c in range(8) for x in (("C", c), ("xc", c))] + [("B", c) for c in range(8)] + \
         [("q", c) for c in range(8)] + [("k", c) for c in range(8)]
COL0 = {"B": 0, "C": 1024, "xc": 2048, "q": 3072, "k": 4096}


class Trk:
    def __init__(self):
        self.ops = {e: [] for e in ("pe", "act", "dve", "pool", "sp")}
        self.cnt = {e: 0 for e in ("pe", "act", "dve", "pool")}
        self.tot = {}
        self.res = {}
        self.seen = {e: {} for e in self.ops}
        self.overlaps = {}
        self.pe_pending = False

    def _r(self, k):
        if k not in self.res:
            self.res[k] = {"w": None, "r": {}}
        return self.res[k]

    def op(self, eng, meth, reads=(), writes=(), inc=True, dma=None, **kw):
        fn = (meth, kw)
        own = eng if eng in self.cnt else None
        waits = {}

        def need(ev, skip_same):
            if ev is None:
                return
            s, v = ev
            if skip_same and s == own and dma is None:
                return
            if self.seen[eng].get(s, 0) >= v:
                return
            if waits.get(s, 0) < v:
                waits[s] = v

        for k in reads:
            need(self._r(k)["w"], False)
        for k in writes:
            for kk in [k] + self.overlaps.get(k, []):
                r = self._r(kk)
                need(r["w"], True)
                for s, v in r["r"].items():
                    need((s, v), True)
        for s, v in waits.items():
            self.seen[eng][s] = v
        rec = {"fn": fn, "waits": list(waits.items()), "inc": None}
        self.ops[eng].append(rec)
        if dma is not None:
            self.tot[dma] = self.tot.get(dma, 0) + 16
            ev = (dma, self.tot[dma])
            rec["inc"] = (dma, 16)
        elif eng == "pe":
            ev = ("pe", self.cnt["pe"] + 1)
            self.pe_pending = True
            if inc:
                self.flush_pe()
        else:
            self.cnt[eng] += 1
            ev = (eng, self.cnt[eng])
            rec["inc"] = (eng, 1)
        for k in writes:
            r = self._r(k)
            r["w"] = ev
            r["r"] = {}
        for k in reads:
            r = self._r(k)
            if r["r"].get(ev[0], 0) < ev[1]:
                r["r"][ev[0]] = ev[1]
        return ev

    def flush_pe(self):
        if self.pe_pending:
            self.ops["pe"][-1]["inc"] = ("pe", 1)
            self.cnt["pe"] += 1
            self.pe_pending = False


def build_program(cfg):
    L = cfg["L"]; DFF = cfg["DFF"]; NF = DFF // 128
    NPS = cfg["NPS"]; SEQ = cfg["SEQ"]; NSS = cfg["NSS"]; TS = cfg["TS"]; CACHE = cfg["CACHE"]
    NT = SEQ // 512
    DIN = 6144
    assert NSS == 2 and TS == 16 and CACHE == 512

    nc = bass.Bass("TRN2", target_bir_lowering=False)

    def din(name, shape, dt=F32):
        return nc.dram_tensor(name, list(shape), dt, kind="ExternalInput")

    def dout(name, shape):
        return nc.dram_tensor(name, list(shape), F32, kind="ExternalOutput")

    xp = din("xp", [NPS, SEQ, D]); xs_in = din("xs", [NSS, TS, D])
    ck = din("ck", [L, NSS, CACHE, 1024]); cv = din("cv", [L, NSS, CACHE, 1024])
    sconv = din("sconv", [L, NSS, 128, 8, 2])
    wgu = [din("wgu1", [L, NF, 128, 4096]), din("wgu2", [L, NF, 128, 4096])]
    wd = [din("wd1", [L, 4, 128, NF * 512]), din("wd2", [L, 4, 128, NF * 512])]
    w_inf = din("w_inf", [L, 20, 128, 4096]); w_int = din("w_int", [L, 2, 2, 128, 8192])
    w_out = din("w_out", [L, 4, 128, 8192])
    gpre_d = din("gpre", [128, L * 3 * 16]); gmix_d = din("gmix", [128, L * 16])
    convw_d = din("convw", [128, L * 24]); gpost_d = din("gpost", [L * 3, D])
    bias_d = din("biasx", [L, 128, 8, 640]); ident_d = din("ident", [128, 128])

    yp = dout("yp", [NPS, SEQ, D]); ys = dout("ys", [NSS, TS, D])
    kp = dout("kp", [L, NPS, 512, 1024]); vp = dout("vp", [L, NPS, 512, 1024])
    cp = dout("cp", [L, NPS, 128, 8, 2])
    ksm = dout("ksm", [L, NSS, TS, 1024]); vsm = dout("vsm", [L, NSS, TS, 1024])
    csm = dout("csm", [L, NSS, 128, 8, 2])
    kvs_k = nc.dram_tensor("kvs_k", [L, 128, 8, 512], BF16)
    kvs_v = nc.dram_tensor("kvs_v", [L, 128, 4, 1024], BF16)

    T = Trk()
    es = contextlib.ExitStack()

    def sb(name, shape, dt):
        return es.enter_context(nc.sbuf_tensor(name, list(shape), dt))

    xres = sb("xres", [128, 4, D], F32)
    xnT = sb("xnT", [128, KD, 512], BF16)
    NB = max(NF * 512, 4608 + 8704 + 10240 + 10240)
    regB = sb("regB", [128, NB], BF16)
    regA = sb("regA", [128, 8224], F32)
    wring = sb("wring", [128, NSLOT, 4096], BF16)
    gpost = sb("gpost_s", [128, D], F32)
    xsb = sb("xsb_s", [128, D], BF16)
    tmp = sb("tmp_s", [128, 4, 512], F32)
    Sb = sb("Sb", [128, 2, 640], F32)
    Pb = sb("Pb", [128, 2, 640], BF16)
    PT = sb("PT", [128, 2, 640], BF16)
    sqt = sb("sqt", [128, 2, 512], BF16)
    stat = sb("stat_s", [128, 64], F32)
    gpre = sb("gpre_s", [128, L * 48], F32)
    gmix = sb("gmix_s", [128, L * 16], F32)
    convw = sb("convw_s", [128, L * 24], F32)
    hist = sb("hist_s", [128, L * 32], F32)
    identf = sb("identf", [128, 128], F32)
    ident = sb("ident_s", [128, 128], BF16)
    ones = sb("ones_s", [128, 128], BF16)
    epsc = sb("epsc", [128, 1], F32)
    psb = [es.enter_context(nc.psum_tensor(f"ps{i}", [128, 512], F32)) for i in range(8)]

    hT = regB[:, 0:NF * 512].rearrange("p (j t) -> p j t", j=NF)
    QTp = regB[:, 0:4096].rearrange("p (h t) -> p h t", h=8)
    QTs = regB[:, 0:512].rearrange("p (h t) -> p h t", h=8)
    kstage = regB[:, 512:4608].rearrange("p (s e) -> p s e", s=4)
    KTcat = regB[:, 4608:4608 + 8704].rearrange("p (h k) -> p h k", h=8)
    Vcat = regB[:, 13312:13312 + 10240].rearrange("p (c e) -> p c e", c=10)
    biasv = regB[:, 23552:23552 + 10240].bitcast(F32).rearrange("p (h j) -> p h j", h=8)
    fbuf = regA[:, 0:8192].rearrange("p (s d) -> p s d", s=4)
    yc = regA[:, 4112:8208].rearrange("p (c t) -> p c t", c=8)
    ya = regA[:, 0:4096].rearrange("p (g e) -> p g e", g=4)

    mixkeys = ["QT", "KTcat", "Vcat", "bias", "kstage"]
    T.overlaps["hT"] = list(mixkeys)
    for k in mixkeys:
        T.overlaps[k] = ["hT"]
    T.overlaps["QT"].append("kstage"); T.overlaps["kstage"].append("QT")
    akeys = [f"gx{c}" for c in range(8)] + [f"yc{c}" for c in range(8)] + [f"ya{g}" for g in range(4)]
    for s in range(4):
        T.overlaps[f"fbuf{s}"] = list(akeys)
    for k in akeys:
        T.overlaps[k] = [f"fbuf{s}" for s in range(4)]
    for g in range(4):
        T.overlaps[f"ya{g}"] += [f"gx{c}" for c in range(8)]
    for c in range(8):
        T.overlaps[f"gx{c}"] += [f"ya{g}" for g in range(4)]

    st_i = [0]

    def statcol():
        st_i[0] = (st_i[0] + 1) % 60
        return st_i[0], f"st{st_i[0]}"

    bank_i = [0]

    def bank():
        bank_i[0] = (bank_i[0] + 1) % 8
        return psb[bank_i[0]], f"ps{bank_i[0]}"

    tmp_i = [0]

    def gettmp():
        tmp_i[0] = (tmp_i[0] + 1) % 4
        return tmp[:, tmp_i[0], :], f"tmp{tmp_i[0]}"

    rot = {"S": 0, "P": 0, "PT": 0, "sq": 0}

    def rot2(k):
        rot[k] ^= 1
        return rot[k]

    def dma_sp(out, in_, sem, reads, writes):
        T.op("sp", "dma_start", out=out, in_=in_, reads=reads, writes=writes, dma=sem)

    def dma_pool(out, in_, sem, reads, writes):
        T.op("pool", "dma_start", out=out, in_=in_, reads=reads, writes=writes, dma=sem)

    wq = []
    wstate = {"issued": 0, "used": 0}

    def w_issue_upto(n):
        while wstate["issued"] < min(n, len(wq)):
            u = wstate["issued"]
            slot = u % NSLOT
            for (dst, src) in wq[u]:
                dma_pool(dst(slot), src, f"w{slot}", reads=(), writes=(f"ws{slot}",))
            wstate["issued"] += 1

    def w_next():
        u = wstate["used"]
        w_issue_upto(u + NSLOT)
        wstate["used"] += 1
        slot = u % NSLOT
        return slot, f"ws{slot}"

    def plan_unit(src_ap, n):
        wq.append([(lambda slot: wring[:, slot, 0:n], src_ap)])

    STOP = cfg.get("stop")

    def plan_tile_layer(l, need_k):
        for f in range(2):
            if f == 1 and STOP == "ffn1":
                return
            if f == 1:
                for i in range(len(INPROJ) // 2):
                    plan_unit(w_inf[l, i], 4096)
                if STOP == "mixA":
                    return
                for sec in ([0] + ([1] if need_k else [])):
                    for c2 in range(2):
                        for u in range(2):
                            plan_unit(w_int[l, sec, c2][:, u * 4096:(u + 1) * 4096], 4096)
                if STOP in ("mixB", "mixBn", "mixC", "mixD"):
                    return
                for c in range(4):
                    for u in range(2):
                        plan_unit(w_out[l, c][:, u * 4096:(u + 1) * 4096], 4096)
                if STOP == "mix":
                    return
            for j in range(NF):
                plan_unit(wgu[f][l, j], 4096)
            for c in range(4):
                for u in range((NF + 7) // 8):
                    nk = min(8, NF - 8 * u)
                    plan_unit(wd[f][l, c][:, 8 * u * 512:(8 * u + nk) * 512], nk * 512)

    def rstd_from(ss_c, ss_k, n, rows):
        c1, k1 = statcol()
        T.op("act", "activation", out=stat[:rows, c1:c1 + 1], in_=stat[:rows, ss_c:ss_c + 1], func=AF.Sqrt,
             bias=epsc[:rows, 0:1], scale=1.0 / n, reads=(ss_k, "epsc"), writes=(k1,))
        c2, k2 = statcol()
        T.op("dve", "reciprocal", out=stat[:rows, c2:c2 + 1], in_=stat[:rows, c1:c1 + 1],
             reads=(k1,), writes=(k2,))
        return c2, k2

    def sumsq(src_ap, src_keys, rows, junk_ap, junk_keys):
        c0, k0 = statcol()
        T.op("dve", "memset", ap=stat[:rows, c0:c0 + 1], constant=0.0, writes=(k0,))
        T.op("act", "activation", out=junk_ap, in_=src_ap, func=AF.Square, accum_out=stat[:rows, c0:c0 + 1],
             reads=tuple(src_keys) + (k0,), writes=tuple(junk_keys) + (k0,))
        return c0, k0

    def transpose_to_xnT(half, rows, nchunk, kbase, tok0, gain_ap):
        pb_, pk = bank()
        pv = pb_[:, :].bitcast(BF16).rearrange("p (k t) -> p k t", k=8)
        xk = f"xsb{half}"
        for kk in range(nchunk):
            c0 = half * 1024 + kk * 128
            T.op("pe", "transpose", out=pv[:, kk, :rows], in_=xsb[:rows, c0:c0 + 128], identity=ident[:rows, :rows],
                 reads=(xk, "ident"), writes=(pk,), inc=(kk == nchunk - 1))
        g_b = gain_ap.unsqueeze(2).to_broadcast([128, nchunk, rows])
        T.op("dve", "tensor_tensor", out=xnT[:, kbase:kbase + nchunk, tok0:tok0 + rows],
             in0=pv[:, 0:nchunk, :rows], in1=g_b, op=ALU.mult, reads=(pk, "consts"), writes=("xnT",))

    def pre1(tl, s, rs):
        rows = tl["PS"]
        c0, k0 = sumsq(xres[:rows, s, :], (f"xres{s}",), rows, fbuf[:rows, s, :], (f"fbuf{s}",))
        rs[s] = rstd_from(c0, k0, D, rows)

    def pre2(tl, l, gi, s, rs):
        rows = tl["PS"]
        c2, k2 = rs[s]
        for half in range(2):
            T.op("act", "activation", out=xsb[:rows, half * 1024:(half + 1) * 1024],
                 in_=xres[:rows, s, half * 1024:(half + 1) * 1024], func=AF.Copy,
                 scale=stat[:rows, c2:c2 + 1], reads=(f"xres{s}", k2), writes=(f"xsb{half}",))
        for half in range(2):
            gb = (l * 3 + gi) * 16 + half * 8
            transpose_to_xnT(half, rows, 8, half * 8, s * 128, gpre[:, gb:gb + 8])

    def prenorm(tl, l, gi):
        rs = {}
        NS_ = tl["NS"]
        for t in range(NS_ + 1):
            if t < NS_:
                pre1(tl, t, rs)
            if 0 <= t - 1 < NS_:
                pre2(tl, l, gi, t - 1, rs)

    def load_gpost(l, gi):
        dma_sp(gpost[:, :], gpost_d[l * 3 + gi:l * 3 + gi + 1, :].partition_broadcast(128), "gp", (), ("gpost",))

    def post1(tl, s, coef):
        rows = tl["PS"]
        c0, k0 = sumsq(fbuf[:rows, s, :], (f"fbuf{s}",), rows, regB[:rows, 0:2048], ("hT",))
        c2, k2 = rstd_from(c0, k0, D, rows)
        T.op("dve", "scalar_tensor_tensor", out=fbuf[:rows, s, :], in0=fbuf[:rows, s, :],
             scalar=stat[:rows, c2:c2 + 1], in1=gpost[:rows, :], op0=ALU.mult, op1=ALU.mult,
             reads=(f"fbuf{s}", k2, "gpost"), writes=(f"fbuf{s}",))
        T.op("dve", "scalar_tensor_tensor", out=xres[:rows, s, :], in0=fbuf[:rows, s, :],
             scalar=coef, in1=xres[:rows, s, :], op0=ALU.mult, op1=ALU.add,
             reads=(f"fbuf{s}", f"xres{s}"), writes=(f"xres{s}",))

    def postnorm(tl, coef=0.5, nxt=None):
        NS_ = tl["NS"]
        if nxt is None:
            for s in range(NS_):
                post1(tl, s, coef)
            return
        rs = {}
        for t in range(NS_ + 2):
            if t < NS_:
                post1(tl, t, coef)
            if 0 <= t - 1 < NS_:
                pre1(tl, t - 1, rs)
            if 0 <= t - 2 < NS_:
                pre2(tl, nxt[0], nxt[1], t - 2, rs)

    evac_i = [0]

    def evac_copy(out, in_, reads, writes):
        evac_i[0] ^= 1
        if evac_i[0]:
            T.op("act", "copy", out=out, in_=in_, reads=reads, writes=writes)
        else:
            T.op("dve", "tensor_copy", out=out, in_=in_, reads=reads, writes=writes)

    def slab_matmul(tl, lhs_fn, lhs_key, nktot, col_slabs):
        rows = tl["PS"]
        for c in range(col_slabs):
            accs = [bank() for _ in range(tl["NS"])]
            k = 0
            while k < nktot:
                nk = min(8, nktot - k)
                slot, wk = w_next()
                wv = wring[:, slot, 0:nk * 512].rearrange("p (k c) -> p k c", k=nk)
                for kk in range(nk):
                    for s in range(tl["NS"]):
                        kq = k + kk
                        last = (kq == nktot - 1)
                        T.op("pe", "matmul", out=accs[s][0][:rows, :], lhsT=lhs_fn(kq, s), rhs=wv[:, kk, :],
                            start=(kq == 0), stop=last,
                             reads=(lhs_key, wk), writes=(accs[s][1],), inc=last)
                T.flush_pe()
                k += nk
            for s in range(tl["NS"]):
                evac_copy(fbuf[:rows, s, c * 512:(c + 1) * 512], accs[s][0][:rows, :],
                          (accs[s][1],), (f"fbuf{s}",))

    def ffn(tl, l, f, do_pre, nxt):
        Tn = tl["T"]
        load_gpost(l, 0 if f == 0 else 2)
        if do_pre:
            prenorm(tl, l, 0 if f == 0 else 2)
        for j in range(NF):
            slot, wk = w_next()
            wv = wring[:, slot, :].rearrange("p (g k c) -> p g k c", g=2, k=16)
            pg, pgk = bank(); pu, puk = bank()
            for g, (pp, ppk) in enumerate(((pg, pgk), (pu, puk))):
                for k in range(KD):
                    T.op("pe", "matmul", out=pp[:, :Tn], lhsT=wv[:, g, k, :], rhs=xnT[:, k, :Tn],
                                                                 start=(k == 0), stop=(k == KD - 1),
                         reads=("xnT", wk), writes=(ppk,), inc=(k == KD - 1))
            tv, tk = gettmp()
            T.op("act", "activation", out=tv[:, :Tn], in_=pg[:, :Tn], func=AF.Silu,
                 reads=(pgk,), writes=(tk,))
            T.op("dve", "tensor_tensor", out=hT[:, j, :Tn], in0=pu[:, :Tn], in1=tv[:, :Tn], op=ALU.mult,
                 reads=(puk, tk), writes=("hT",))
        slab_matmul(tl, lambda kq, s: hT[:, kq, s * 128:s * 128 + tl["PS"]], "hT", NF, 4)
        postnorm(tl, 0.5, nxt)

    def mix(tl, l, nxt):
        Tn = tl["T"]; nseq = tl["nseq"]; Ls = tl["Lseg"]; samp = tl["samp"]
        QT = QTs if samp else QTp
        Wg = 2 + Ls
        gx = regA[:, 0:8 * nseq * Wg].rearrange("p (c a t) -> p c a t", c=8, a=nseq)
        load_gpost(l, 1)
        dma_sp(biasv, bias_d[l], "bias", (), ("bias",))
        T.op("dve", "memset", ap=biasv[0:64, :, 576:640], constant=NEG, writes=("bias",))
        T.op("dve", "memset", ap=biasv[64:128, :, 0:64], constant=NEG, writes=("bias",))
        if samp:
            T.op("dve", "memset", ap=biasv[0:32, :, 528:544], constant=NEG, writes=("bias",))
        hb = l * 32
        hv = hist[:, hb:hb + 32].rearrange("p (a c r) -> p a c r", a=2, c=8)
        if samp:
            for a in range(nseq):
                dma_pool(Vcat[:, 5 * a:5 * a + 4, :], cv[l, a].rearrange("(s p) e -> p s e", p=128), "cv", (), ("Vcat",))
                dma_pool(kstage, ck[l, a].rearrange("(s p) e -> p s e", p=128), "ck", (), ("kstage",))
                for h in range(8):
                    if h % 2 == 0:
                        pb_, pk = bank()
                        pv = pb_[:, :].bitcast(BF16).rearrange("p (k t) -> p k t", k=8)
                    for s4 in range(4):
                        T.op("pe", "transpose", out=pv[:, (h % 2) * 4 + s4, :], in_=kstage[:, s4, h * 128:(h + 1) * 128], identity=ident[:, :],
                             reads=("kstage", "ident"), writes=(pk,), inc=(h % 2 == 1 and s4 == 3))
                    if h % 2 == 1:
                        evac_copy(KTcat[:, h - 1:h + 1, a * 544:a * 544 + 512],
                                  pv[:, :, :].rearrange("p (h s) t -> p h (s t)", h=2), (pk,), ("KTcat",))
                dma_sp(hv[:, a, :, :], sconv[l, a], "hist", (), ("hist",))
        elif tl["ti"] > 0:
            dma_sp(KTcat[:, :, 0:512], kvs_k[l], "kvlk", (f"kvsk{l}",), ("KTcat",))
            dma_sp(Vcat[:, 0:4, :], kvs_v[l], "kvlv", (f"kvsv{l}",), ("Vcat",))
        else:
            T.op("dve", "memset", ap=hist[:, hb:hb + 32], constant=0.0, writes=("hist",))
        cwv = convw[:, l * 24:(l + 1) * 24].rearrange("p (c j) -> p c j", c=8)
        ctmp = {}
        for i, (kind, c) in enumerate(INPROJ):
            if i % 2 == 0:
                slot, wk = w_next()
                wv = wring[:, slot, :].rearrange("p (g k c) -> p g k c", g=2, k=16)
            g = i % 2
            pp, ppk = bank()
            for k in range(KD):
                T.op("pe", "matmul", out=pp[:, :Tn], lhsT=wv[:, g, k, :], rhs=xnT[:, k, :Tn],
                                                                    start=(k == 0), stop=(k == KD - 1),
                     reads=("xnT", wk), writes=(ppk,), inc=(k == KD - 1))
            if kind == "C":
                tv, tk = gettmp()
                ctmp[c] = (tv, tk)
                T.op("act", "copy", out=tv[:, :Tn], in_=pp[:, :Tn], reads=(ppk,), writes=(tk,))
            elif kind == "xc":
                tv, tk = ctmp[c]
                for a in range(nseq):
                    T.op("act", "copy", out=gx[:, c, a, 0:2], in_=hv[:, a, c, :],
                         reads=("hist",), writes=(f"gx{c}",))
                    T.op("dve", "tensor_tensor", out=gx[:, c, a, 2:2 + Ls], in0=pp[:, a * Ls:(a + 1) * Ls], in1=tv[:, a * Ls:(a + 1) * Ls], op=ALU.mult,
                         reads=(ppk, tk), writes=(f"gx{c}",))
                    ycv = yc[:, c, a * Ls:(a + 1) * Ls]
                    T.op("dve", "tensor_scalar", out=ycv, in0=gx[:, c, a, 0:Ls], scalar1=cwv[:, c, 0:1], scalar2=None, op0=ALU.mult,
                         reads=(f"gx{c}", "consts"), writes=(f"yc{c}",))
                    for jj in (1, 2):
                        T.op("dve", "scalar_tensor_tensor", out=ycv, in0=gx[:, c, a, jj:jj + Ls], scalar=cwv[:, c, jj:jj + 1], in1=ycv,
                            op0=ALU.mult, op1=ALU.add,
                             reads=(f"gx{c}", f"yc{c}", "consts"), writes=(f"yc{c}",))
                if c == 7:
                    for a in range(nseq):
                        for cc in range(8):
                            T.op("act", "copy", out=hv[:, a, cc, :], in_=gx[:, cc, a, tl["Lreal"]:tl["Lreal"] + 2],
                                 reads=(f"gx{cc}",), writes=("hist",))
                    if samp:
                        for a in range(nseq):
                            dma_sp(csm[l, a], hv[:, a, :, :], "cst", ("hist",), ())
                    elif tl["last"]:
                        dma_sp(cp[l, tl["b"]], hv[:, 0, :, :], "cst", ("hist",), ())
            elif kind == "B":
                T.op("dve", "tensor_tensor", out=yc[:, c, :Tn], in0=pp[:, :Tn], in1=yc[:, c, :Tn], op=ALU.mult,
                     reads=(ppk, f"yc{c}"), writes=(f"yc{c}",))
            elif kind == "q":
                T.op("act", "copy", out=QT[:, c, :Tn], in_=pp[:, :Tn], reads=(ppk,), writes=("QT",))
            else:
                if samp:
                    for a in range(nseq):
                        T.op("act", "copy", out=KTcat[:, c, a * 544 + 512:a * 544 + 544],
                                                                      in_=pp[:, a * Ls:(a + 1) * Ls],
                             reads=(ppk,), writes=("KTcat",))
                else:
                    T.op("act", "copy", out=KTcat[:, c, 512:1024], in_=pp[:, :Tn],
                         reads=(ppk,), writes=("KTcat",))
        if STOP == "mixA":
            return
        groups = tl["groups"]
        need_out = samp or tl["last"]
        for sec in (["v"] + (["k"] if need_out else [])):
            for c2 in range(2):
                accs = [bank() for _ in groups]
                for u in range(2):
                    slot, wk = w_next()
                    wv = wring[:, slot, :].rearrange("p (k c) -> p k c", k=8)
                    for kk in range(8):
                        kq = 8 * u + kk
                        for gi, (t0, nr) in enumerate(groups):
                            T.op("pe", "matmul", out=accs[gi][0][:nr, :], lhsT=xnT[:, kq, t0:t0 + nr], rhs=wv[:, kk, :],
                                start=(kq == 0), stop=(kq == KD - 1),
                                 reads=("xnT", wk), writes=(accs[gi][1],), inc=(kq == KD - 1))
                    T.flush_pe()
                for gi, (t0, nr) in enumerate(groups):
                    pa, pak = accs[gi]
                    vch = (5 * gi + 4) if samp else (4 + gi)
                    outp = need_out and STOP != "mixBn"
                    if outp:
                        tv, tk = gettmp()
                        T.op("act", "copy", out=tv[:nr, :], in_=pa[:nr, :], reads=(pak,), writes=(tk,))
                        if sec == "v":
                            T.op("dve", "tensor_copy", out=Vcat[:nr, vch, c2 * 512:(c2 + 1) * 512], in_=tv[:nr, :],
                                 reads=(tk,), writes=("Vcat",))
                    elif sec == "v":
                        T.op("act", "copy", out=Vcat[:nr, vch, c2 * 512:(c2 + 1) * 512], in_=pa[:nr, :],
                             reads=(pak,), writes=("Vcat",))
                    if outp:
                        if samp:
                            nr = TS
                            dst = (vsm if sec == "v" else ksm)[l, gi, :, c2 * 512:(c2 + 1) * 512]
                        else:
                            dst = (vp if sec == "v" else kp)[l, tl["b"], t0:t0 + nr, c2 * 512:(c2 + 1) * 512]
                        dma_sp(dst, tv[:nr, :], "ost", (tk,), ("ostq",))
        if (not samp) and (not tl["last"]):
            dma_sp(kvs_k[l], KTcat[:, :, 512:1024], "kvstk", ("KTcat",), (f"kvsk{l}",))
            dma_sp(kvs_v[l], Vcat[:, 4:8, :], "kvstv", ("Vcat",), (f"kvsv{l}",))
        if STOP in ("mixB", "mixBn"):
            return
        def conv_norm():
            pn, pnk = bank()
            for c in range(8):
                r = rot2("sq")
                T.op("act", "activation", out=sqt[:, r, :Tn], in_=yc[:, c, :Tn], func=AF.Square,
                     reads=(f"yc{c}",), writes=(f"sq{r}",))
                T.op("pe", "matmul", out=pn[:, :Tn], lhsT=ones[:, :], rhs=sqt[:, r, :Tn], start=(c == 0), stop=(c == 7),
                     reads=(f"sq{r}", "ones"), writes=(pnk,), inc=True)
            rv, rk = gettmp()
            T.op("act", "activation", out=rv[:, :Tn], in_=pn[:, :Tn], func=AF.Sqrt, bias=epsc[:, 0:1], scale=1.0 / DCONV,
                 reads=(pnk, "epsc"), writes=(rk,))
            T.op("dve", "reciprocal", out=rv[:, :Tn], in_=rv[:, :Tn], reads=(rk,), writes=(rk,))
            for c in range(8):
                T.op("dve", "scalar_tensor_tensor", out=xnT[:, c, :Tn], in0=yc[:, c, :Tn],
                                                                scalar=gmix[:, l * 16 + c:l * 16 + c + 1], in1=rv[:, :Tn],
                                                                op0=ALU.mult, op1=ALU.mult,
                     reads=(f"yc{c}", rk, "consts"), writes=("xnT",))
        if STOP == "mixC":
            return
        items = []
        for gi, (t0, nq) in enumerate(groups):
            if samp:
                ks, ke, jo = 544 * gi, 544 * gi + 544, 0
            else:
                ks = 512 if tl["ti"] == 0 else 128 * gi
                ke = 128 * gi + 640
                jo = ks - 128 * gi
            for h in range(8):
                items.append(dict(gi=gi, t0=t0, nq=nq, ks=ks, ke=ke, jo=jo, h=h, nk=ke - ks))

        def st1(i):
            it = items[i]; nq = it["nq"]; nk = it["nk"]; h = it["h"]; ks = it["ks"]; ke = it["ke"]; jo = it["jo"]; t0 = it["t0"]
            n1 = min(512, nk); n2 = nk - n1
            r = i % 2
            b1, b1k = bank()
            T.op("pe", "matmul", out=b1[:nq, :n1], lhsT=QT[:, h, t0:t0 + nq], rhs=KTcat[:, h, ks:ks + n1],
                 start=True, stop=True, reads=("QT", "KTcat"), writes=(b1k,), inc=True)
            if n2:
                b2, b2k = bank()
                T.op("pe", "matmul", out=b2[:nq, :n2], lhsT=QT[:, h, t0:t0 + nq], rhs=KTcat[:, h, ks + n1:ke],
                     start=True, stop=True, reads=("QT", "KTcat"), writes=(b2k,), inc=True)
            T.op("dve", "scalar_tensor_tensor", out=Sb[:nq, r, 0:n1], in0=b1[:nq, :n1], scalar=SCALE,
                 in1=biasv[:nq, h, jo:jo + n1], op0=ALU.mult, op1=ALU.add, reads=(b1k, "bias"), writes=(f"S{r}",))
            if n2:
                T.op("dve", "scalar_tensor_tensor", out=Sb[:nq, r, n1:nk], in0=b2[:nq, :n2], scalar=SCALE,
                     in1=biasv[:nq, h, jo + n1:jo + nk], op0=ALU.mult, op1=ALU.add, reads=(b2k, "bias"), writes=(f"S{r}",))
            cm, km = statcol()
            T.op("dve", "reduce_max", out=stat[:nq, cm:cm + 1], in_=Sb[:nq, r, 0:nk], axis=AX.X,
                 reads=(f"S{r}",), writes=(km,))
            cn, kn = statcol()
            T.op("dve", "tensor_scalar", out=stat[:nq, cn:cn + 1], in0=stat[:nq, cm:cm + 1],
                 scalar1=-1.0, scalar2=None, op0=ALU.mult, reads=(km,), writes=(kn,))
            csu, ksu = statcol()
            T.op("dve", "memset", ap=stat[:nq, csu:csu + 1], constant=0.0, writes=(ksu,))
            T.op("act", "activation", out=Pb[:nq, r, 0:nk], in_=Sb[:nq, r, 0:nk], func=AF.Exp,
                 bias=stat[:nq, cn:cn + 1], scale=1.0, accum_out=stat[:nq, csu:csu + 1],
                 reads=(f"S{r}", kn, ksu), writes=(f"P{r}", ksu))
            it["csu"] = csu; it["ksu"] = ksu

        def st2(i):
            it = items[i]; nq = it["nq"]; nk = it["nk"]
            r = i % 2
            nch = (nk + 127) // 128
            pb_, pk = bank()
            pv = pb_[:, :].bitcast(BF16).rearrange("p (k t) -> p k t", k=8)
            for ci in range(nch):
                csz = min(128, nk - ci * 128)
                T.op("pe", "transpose", out=pv[:csz, ci, :nq], in_=Pb[:nq, r, ci * 128:ci * 128 + csz],
                     identity=ident[:nq, :nq], reads=(f"P{r}", "ident"), writes=(pk,), inc=(ci == nch - 1))
            PTv = PT[:, r, :].rearrange("p (k t) -> p k t", k=5)
            nfull = nk // 128
            T.op("act", "copy", out=PTv[:, 0:nfull, :nq], in_=pv[:, 0:nfull, :nq], reads=(pk,), writes=(f"PT{r}",))
            if nfull < nch:
                cs = nk - nfull * 128
                T.op("act", "copy", out=PTv[:cs, nfull, :nq], in_=pv[:cs, nfull, :nq], reads=(pk,), writes=(f"PT{r}",))

        def st3(i):
            it = items[i]; nq = it["nq"]; nk = it["nk"]; h = it["h"]; gi = it["gi"]; ks = it["ks"]; t0 = it["t0"]
            r = i % 2
            nch = (nk + 127) // 128
            PTv = PT[:, r, :].rearrange("p (k t) -> p k t", k=5)
            po, pok = bank()
            for ci in range(nch):
                csz = min(128, nk - ci * 128)
                vch = (5 * gi + ci) if samp else (ks // 128 + ci)
                T.op("pe", "matmul", out=po[:nq, 0:128], lhsT=PTv[:csz, ci, :nq], rhs=Vcat[:csz, vch, h * 128:(h + 1) * 128],
                     start=(ci == 0), stop=(ci == nch - 1), reads=(f"PT{r}", "Vcat"), writes=(pok,), inc=(ci == nch - 1))
            cr, kr = statcol()
            T.op("dve", "reciprocal", out=stat[:nq, cr:cr + 1], in_=stat[:nq, it["csu"]:it["csu"] + 1],
                 reads=(it["ksu"],), writes=(kr,))
            T.op("dve", "tensor_scalar", out=ya[:nq, gi, h * 128:(h + 1) * 128], in0=po[:nq, 0:128],
                 scalar1=stat[:nq, cr:cr + 1], scalar2=None, op0=ALU.mult, reads=(pok, kr), writes=(f"ya{gi}",))
            if h == 7:
                pending.append((cur_step[0] + 1, "a", gi, nq, t0))
                pending.append((cur_step[0] + 4, "b", gi, nq, t0))

        def ya_norm(part, gi, nq, t0):
            if part == "a":
                c0, k0 = sumsq(ya[:nq, gi, :], (f"ya{gi}",), nq, xsb[:nq, 1024:2048], ("xsb1",))
                c2, k2 = rstd_from(c0, k0, 1024, nq)
                T.op("act", "activation", out=xsb[:nq, 0:1024], in_=ya[:nq, gi, :], func=AF.Copy,
                     scale=stat[:nq, c2:c2 + 1], reads=(f"ya{gi}", k2), writes=("xsb0",))
            else:
                transpose_to_xnT(0, nq, 8, 8, t0, gmix[:, l * 16 + 8:l * 16 + 16])

        nit = len(items)
        pending = []
        cur_step = [0]
        for i in range(nit + 2):
            cur_step[0] = i
            if i < nit:
                st1(i)
            if 0 <= i - 1 < nit:
                st2(i - 1)
            if 0 <= i - 2 < nit:
                st3(i - 2)
            if i == 4:
                conv_norm()
            pending.sort(key=lambda x: (x[0], x[1]))
            while pending and pending[0][0] <= i:
                _, part_, g_, nq_, t0_ = pending.pop(0)
                ya_norm(part_, g_, nq_, t0_)
        for _, part_, g_, nq_, t0_ in sorted(pending, key=lambda x: (x[0], x[1])):
            ya_norm(part_, g_, nq_, t0_)
        if STOP == "mixD":
            return
        slab_matmul(tl, lambda kq, s: xnT[:, kq, s * 128:s * 128 + tl["PS"]], "xnT", KD, 4)
        postnorm(tl, 1.0, nxt)

    tiles = []
    for b in range(NPS):
        for ti in range(NT):
            tiles.append(dict(samp=False, b=b, ti=ti, last=(ti == NT - 1), T=512, NS=4, PS=128, nseq=1, Lseg=512, Lreal=512,
                              groups=[(i * 128, 128) for i in range(4)]))
    tiles.append(dict(samp=True, b=0, ti=0, last=True, T=64, NS=1, PS=64, nseq=NSS, Lseg=32, Lreal=TS,
                      groups=[(a * 32, 32) for a in range(NSS)]))
    for tl in tiles:
        for l in range(L):
            plan_tile_layer(l, tl["samp"] or tl["last"])

    dma_sp(identf[:, :], ident_d[:, :], "cst0", (), ("consts",))
    dma_sp(gpre[:, :], gpre_d[:, :], "cst0", (), ("consts",))
    dma_sp(gmix[:, :], gmix_d[:, :], "cst0", (), ("consts",))
    dma_sp(convw[:, :], convw_d[:, :], "cst0", (), ("consts",))
    T.op("dve", "tensor_copy", out=ident[:, :], in_=identf[:, :], reads=("consts",), writes=("ident",))
    T.op("dve", "memset", ap=ones[:, :], constant=1.0, writes=("ones",))
    T.op("dve", "memset", ap=epsc[:, :], constant=EPS, writes=("epsc",))

    for tl in tiles:
        xk = tuple(f"xres{s}" for s in range(tl["NS"]))
        if tl["samp"]:
            T.op("dve", "memset", ap=xres[:64, 0, :], constant=0.0, writes=xk)
            for a in range(NSS):
                dma_sp(xres[32 * a:32 * a + TS, 0, :], xs_in[a], "ldx", (), xk)
        else:
            t0 = tl["ti"] * 512
            dma_sp(xres[:, :, :], xp[tl["b"], t0:t0 + 512, :].rearrange("(s p) d -> p s d", p=128), "ldx", (), xk)
        for l in range(L):
            ffn(tl, l, 0, l == 0, (l, 1))
            mix(tl, l, (l, 2))
            ffn(tl, l, 1, False, ((l + 1, 0) if l + 1 < L else None))
        if tl["samp"]:
            for a in range(NSS):
                dma_sp(ys[a], xres[32 * a:32 * a + TS, 0, :], "sty", xk, ())
        else:
            t0 = tl["ti"] * 512
            dma_sp(yp[tl["b"], t0:t0 + 512, :].rearrange("(s p) d -> p s d", p=128), xres[:, :, :], "sty", xk, ())
    assert wstate["used"] == len(wq), (wstate, len(wq))
    T.flush_pe()

    semnames = set(T.cnt) | set(T.tot)
    sems = {n: es.enter_context(nc.semaphore("s_" + n)) for n in sorted(semnames)}
    engs = {"pe": "tensor", "act": "scalar", "dve": "vector", "pool": "gpsimd", "sp": "sync"}
    with nc.Block() as block:
        for en, attr in engs.items():
            def body(e, en=en):
                for rec in T.ops[en]:
                    for s, v in rec["waits"]:
                        e.wait_ge(sems[s], v)
                    ins = getattr(e, rec["fn"][0])(**rec["fn"][1])
                    if rec["inc"] is not None:
                        ins.then_inc(sems[rec["inc"][0]], rec["inc"][1])
                if en == "sp":
                    for s, v in T.tot.items():
                        e.wait_ge(sems[s], v)
            getattr(block, attr)(body)
    es.close()
    return nc


def host_inputs(inp, cfg, core):
    L = cfg["L"]; NPS = cfg["NPS"]; NSS = cfg["NSS"]
    f = lambda a: np.ascontiguousarray(np.asarray(a, dtype=np.float32))
    m = {}
    m["xp"] = f(inp["x_prompt"][core * NPS:(core + 1) * NPS])
    m["xs"] = f(inp["x_sample"][core * NSS:(core + 1) * NSS])
    m["ck"] = f(np.asarray(inp["cache_k"])[:, core * NSS:(core + 1) * NSS].reshape(L, NSS, 512, 1024))
    m["cv"] = f(np.asarray(inp["cache_v"])[:, core * NSS:(core + 1) * NSS].reshape(L, NSS, 512, 1024))
    sc = np.asarray(inp["state_conv"])[:, core * NSS:(core + 1) * NSS]
    m["sconv"] = f(sc.reshape(L, NSS, 2, 8, 128).transpose(0, 1, 4, 3, 2))
    return m


def shared_inputs(inp, cfg):
    L = cfg["L"]; NF = cfg["DFF"] // 128
    f = lambda a: np.ascontiguousarray(np.asarray(a, dtype=np.float32))
    A = lambda k: np.asarray(inp[k], dtype=np.float32)
    m = {}

    def gu(wg_, wu_):
        a = wg_.reshape(L, 16, 128, NF, 128).transpose(0, 3, 2, 1, 4)
        b = wu_.reshape(L, 16, 128, NF, 128).transpose(0, 3, 2, 1, 4)
        return np.stack([a, b], axis=3).reshape(L, NF, 128, 4096)

    def slabs(w, nk):
        return f(w.reshape(L, nk, 128, 4, 512).transpose(0, 3, 2, 1, 4)).reshape(L, 4, 128, nk * 512)

    m["wgu1"] = gu(A("ffn1_w_gate"), A("ffn1_w_up")); m["wgu2"] = gu(A("ffn2_w_gate"), A("ffn2_w_up"))
    m["wd1"] = slabs(A("ffn1_w_down"), NF); m["wd2"] = slabs(A("ffn2_w_down"), NF)
    m["w_out"] = slabs(A("w_out"), 16)
    wi = A("w_in")
    a = wi[:, :, 0:5120].reshape(L, 16, 128, 40, 128).transpose(0, 3, 2, 1, 4)
    order = [COL0[kind] // 128 + c for kind, c in INPROJ]
    a = a[:, order]
    m["w_inf"] = f(a.reshape(L, 20, 2, 128, 16, 128).transpose(0, 1, 3, 2, 4, 5)).reshape(L, 20, 128, 4096)
    secs = [wi[:, :, s0:s0 + 1024].reshape(L, 16, 128, 2, 512).transpose(0, 3, 2, 1, 4) for s0 in (5120, 4096)]
    m["w_int"] = np.stack(secs, axis=1).reshape(L, 2, 2, 128, 8192)
    pre = np.stack([np.asarray(inp["ln_ffn1_pre"]), np.asarray(inp["ln_mix_pre"]), np.asarray(inp["ln_ffn2_pre"])], 1)
    m["gpre"] = f(pre.reshape(L, 3, 16, 128).transpose(3, 0, 1, 2).reshape(128, L * 48))
    gm = np.concatenate([np.asarray(inp["g_conv_out"]), np.asarray(inp["g_attn_out"])], 1)
    m["gmix"] = f(gm.reshape(L, 16, 128).transpose(2, 0, 1).reshape(128, L * 16))
    cw = np.asarray(inp["conv_w"])
    m["convw"] = f(cw.reshape(L, 3, 8, 128).transpose(3, 0, 2, 1).reshape(128, L * 24))
    post = np.stack([np.asarray(inp["ln_ffn1_post"]), np.asarray(inp["ln_mix_post"]), np.asarray(inp["ln_ffn2_post"])], 1)
    m["gpost"] = f(post.reshape(L * 3, D))
    q = np.arange(128)[:, None]; j = np.arange(640)[None, :]
    idx = np.clip(q + 512 - j, -128, 128) + 128
    rb = np.asarray(inp["rel_bias"])
    m["biasx"] = f(rb[:, idx, :].transpose(0, 1, 3, 2))
    m["ident"] = np.eye(128, dtype=np.float32)
    return m


_CACHE = {}


def run(inp, cfg, n_cores):
    key = tuple(sorted((k, str(v)) for k, v in cfg.items()))
    if key not in _CACHE:
        _CACHE[key] = build_program(cfg)
    nc = _CACHE[key]
    sh = shared_inputs(inp, cfg)
    in_maps = []
    for c in range(n_cores):
        m = dict(sh)
        m.update(host_inputs(inp, cfg, c))
        in_maps.append(m)
    res = run_bass_kernel_spmd(nc, in_maps, core_ids=list(range(n_cores)))
    R = res.results
    L = cfg["L"]
    cat = lambda k, ax: np.concatenate([r[k] for r in R], axis=ax)
    y_p = cat("yp", 0); y_s = cat("ys", 0)
    k_p = cat("kp", 1).reshape(L, -1, 512, 8, 128); v_p = cat("vp", 1).reshape(L, -1, 512, 8, 128)
    k_s = cat("ksm", 1).reshape(L, -1, cfg["TS"], 8, 128); v_s = cat("vsm", 1).reshape(L, -1, cfg["TS"], 8, 128)
    unconv = lambda a: np.ascontiguousarray(a.transpose(0, 1, 4, 3, 2)).reshape(a.shape[0], a.shape[1], 2, 1024)
    c_p = unconv(cat("cp", 1)); c_s = unconv(cat("csm", 1))
    return (y_p, y_s, k_p, v_p, c_p, k_s, v_s, c_s)


FULL = dict(L=4, DFF=5504, NPS=2, SEQ=2048, NSS=2, TS=16, CACHE=512)


def kernel(**inputs):
    return run(inputs, FULL, 8)
```
